# Optimizing a Trainium2 kernel written in Bass

```python
import math
import jax, jax.numpy as jnp
from jax import lax
import numpy as np

D_MODEL = 1024
BATCH = 8
SEQ = 8192
DEPTH = 1

HEAD_DIM = 64
N_Q_HEADS = 8
N_KV_HEADS = 2
GQA_GROUP = N_Q_HEADS // N_KV_HEADS
D_ATTN = N_Q_HEADS * HEAD_DIM
D_KV = N_KV_HEADS * HEAD_DIM
N_CONV_GROUPS = 8
D_CONV = D_MODEL - D_ATTN
D_MIX = D_ATTN + D_CONV
D_IN = D_ATTN + 2 * D_KV + 3 * D_CONV
WINDOW = 128
BLOCK = 128
CONV_WIDTH = 3
D_FF = 2816
ROPE_THETA = 10000.0
EPS = 1e-6
N_MOD = 6

kernel_name = "hybrid_swa_shortconv_convffn_adaln"


def rms_norm(x, g):
    xf = x.astype(jnp.float32)
    y = xf * lax.rsqrt(jnp.mean(xf * xf, axis=-1, keepdims=True) + EPS)
    return (y * g.astype(jnp.float32)).astype(x.dtype)


def causal_dwconv3(x, w, b=None):
    s = x.shape[1]
    xp = jnp.pad(x, ((0, 0), (CONV_WIDTH - 1, 0), (0, 0)))
    y = xp[:, 0:s] * w[0] + xp[:, 1:s + 1] * w[1] + xp[:, 2:s + 2] * w[2]
    if b is not None:
        y = y + b
    return y


def rope(x, positions):
    half = HEAD_DIM // 2
    inv_freq = ROPE_THETA ** (-jnp.arange(0, half, dtype=jnp.float32) / half)
    ang = positions.astype(jnp.float32)[..., None] * inv_freq
    cos = jnp.cos(ang)[:, :, None, :]
    sin = jnp.sin(ang)[:, :, None, :]
    xf = x.astype(jnp.float32)
    x1, x2 = xf[..., :half], xf[..., half:]
    out = jnp.concatenate([x1 * cos - x2 * sin, x2 * cos + x1 * sin], axis=-1)
    return out.astype(x.dtype)


def sliding_window_attention(q, k, v, sinks):
    bsz, s = q.shape[0], q.shape[1]
    nb = s // BLOCK
    qb = q.reshape(bsz, nb, BLOCK, N_KV_HEADS, GQA_GROUP, HEAD_DIM)
    kb = k.reshape(bsz, nb, BLOCK, N_KV_HEADS, HEAD_DIM)
    vb = v.reshape(bsz, nb, BLOCK, N_KV_HEADS, HEAD_DIM)
    pad = ((0, 0), (1, 0), (0, 0), (0, 0), (0, 0))
    keys = jnp.concatenate([jnp.pad(kb, pad)[:, :-1], kb], axis=2)
    vals = jnp.concatenate([jnp.pad(vb, pad)[:, :-1], vb], axis=2)
    scores = jnp.einsum('bnqhgd,bnkhd->bnhgqk', qb, keys).astype(jnp.float32)
    scores = scores * (1.0 / math.sqrt(HEAD_DIM))
    qi = jnp.arange(BLOCK)[:, None]
    kj = jnp.arange(2 * BLOCK)[None, :]
    diff = qi + BLOCK - kj
    band = (diff >= 0) & (diff < WINDOW)
    blk = jnp.arange(nb)[:, None, None]
    valid = band[None] & ((blk > 0) | (kj[None] >= BLOCK))
    scores = jnp.where(valid[None, :, None, None], scores, -1e30)
    sink = sinks.astype(jnp.float32).reshape(1, 1, N_KV_HEADS, GQA_GROUP, 1, 1)
    sink = jnp.broadcast_to(sink, scores.shape[:-1] + (1,))
    probs = jax.nn.softmax(jnp.concatenate([scores, sink], axis=-1), axis=-1)[..., :-1]
    out = jnp.einsum('bnhgqk,bnkhd->bnqhgd', probs.astype(v.dtype), vals)
    return out.reshape(bsz, s, D_ATTN)


def setup_inputs(seed: int = 0) -> dict:
    key = jax.random.key(seed)
    ks = jax.random.split(key, 20)
    f32 = jnp.float32
    nrm = lambda k, shape, scale: (jax.random.normal(k, shape, f32) * scale)
    gain = lambda k, shape: 1.0 + 0.05 * jax.random.normal(k, shape, f32)
    L = DEPTH
    return {
        "x": jax.random.normal(ks[0], (BATCH, SEQ, D_MODEL), f32),
        "c": jax.random.normal(ks[1], (BATCH, D_MODEL), f32),
        "positions": jnp.arange(SEQ, dtype=jnp.int32)[None, :]
                     + jax.random.randint(ks[2], (BATCH, 1), 0, 4096, dtype=jnp.int32),
        "ada_w": nrm(ks[3], (L, D_MODEL, N_MOD * D_MODEL), 0.5 * D_MODEL ** -0.5),
        "ada_b": nrm(ks[4], (L, N_MOD * D_MODEL), 0.02),
        "norm1_g": gain(ks[5], (L, D_MODEL)),
        "w_in": nrm(ks[6], (L, D_MODEL, D_IN), D_MODEL ** -0.5),
        "b_in": nrm(ks[7], (L, D_IN), 0.02),
        "conv_w": nrm(ks[8], (L, CONV_WIDTH, D_CONV), CONV_WIDTH ** -0.5),
        "attn_sinks": nrm(ks[9], (L, N_Q_HEADS), 0.5),
        "out_norm_attn_g": gain(ks[10], (L, D_ATTN)),
        "out_norm_conv_g": gain(ks[11], (L, D_CONV)),
        "w_o": nrm(ks[12], (L, D_MIX, D_MODEL), D_MIX ** -0.5),
        "norm2_g": gain(ks[13], (L, D_MODEL)),
        "w_up": nrm(ks[14], (L, D_MODEL, 2 * D_FF), D_MODEL ** -0.5),
        "ffn_conv_w": nrm(ks[15], (L, CONV_WIDTH, D_FF), CONV_WIDTH ** -0.5),
        "ffn_conv_b": nrm(ks[16], (L, D_FF), 0.02),
        "w_down": nrm(ks[17], (L, D_FF, D_MODEL), D_FF ** -0.5),
        "final_norm_g": gain(ks[18], (D_MODEL,)),
    }


def reference(x, c, positions, ada_w, ada_b, norm1_g, w_in, b_in, conv_w, attn_sinks,
              out_norm_attn_g, out_norm_conv_g, w_o, norm2_g, w_up, ffn_conv_w,
              ffn_conv_b, w_down, final_norm_g):
    bsz, s, _ = x.shape
    c_act = jax.nn.silu(c)
    for l in range(DEPTH):
        mod = c_act @ ada_w[l] + ada_b[l]
        sh1, sc1, g1, sh2, sc2, g2 = [m[:, None, :] for m in jnp.split(mod, N_MOD, axis=-1)]

        h = rms_norm(x, norm1_g[l]) * (1.0 + sc1) + sh1
        proj = h @ w_in[l] + b_in[l]
        q, k, v, gb, gc, xs = jnp.split(
            proj, np.cumsum([D_ATTN, D_KV, D_KV, D_CONV, D_CONV]).tolist(), axis=-1)
        q = rope(q.reshape(bsz, s, N_Q_HEADS, HEAD_DIM), positions)
        k = rope(k.reshape(bsz, s, N_KV_HEADS, HEAD_DIM), positions)
        v = v.reshape(bsz, s, N_KV_HEADS, HEAD_DIM)
        y_attn = sliding_window_attention(q, k, v, attn_sinks[l])
        y_conv = gc * causal_dwconv3(gb * xs, conv_w[l])
        y = jnp.concatenate([rms_norm(y_attn, out_norm_attn_g[l]),
                             rms_norm(y_conv, out_norm_conv_g[l])], axis=-1)
        x = x + g1 * (y @ w_o[l])

        h = rms_norm(x, norm2_g[l]) * (1.0 + sc2) + sh2
        u = h @ w_up[l]
        gate, val = u[..., :D_FF], u[..., D_FF:]
        f = jax.nn.silu(causal_dwconv3(gate, ffn_conv_w[l], ffn_conv_b[l])) * val
        x = x + g2 * (f @ w_down[l])
    return rms_norm(x, final_norm_g)
```

```python
import math
from contextlib import ExitStack

import numpy as np
import concourse.bass as bass
import concourse.mybir as mybir
from concourse.bass_utils import run_bass_kernel_spmd

F32 = mybir.dt.float32
BF16 = mybir.dt.bfloat16
I32 = mybir.dt.int32
AF = mybir.ActivationFunctionType
ALU = mybir.AluOpType

D = 1024
DFF = 2816
NJ = DFF // 128
T = 512
EPS = 1e-6
C1 = 6.28125
C2 = 2 * math.pi - 6.28125
NWI = 2
NWU = 3
NWD = 8
A_KIND = ["q", "q", "q", "q", "k"] + ["xs", "gb", "gc"] * 4


class Sem:
    def __init__(self, handle, name):
        self.handle = handle
        self.name = name
        self.count = 0


class Eng:
    def __init__(self, name, sem, same_sync):
        self.name = name
        self.sem = sem
        self.items = []
        self.known = {}
        self.same_sync = same_sync


class Sched:
    def __init__(self, nc, stack):
        self.nc = nc
        self.stack = stack
        self.cells = {}
        self.nsem = 0

    def newsem(self, name):
        self.nsem += 1
        return Sem(self.stack.enter_context(self.nc.semaphore(name)), name)

    def neweng(self, name, same_sync):
        return Eng(name, self.newsem("c_" + name), same_sync)

    def _st(self, cell):
        st = self.cells.get(cell)
        if st is None:
            st = {"w": None, "r": {}}
            self.cells[cell] = st
        return st

    def op(self, eng, fn, r=(), w=(), dsem=None, group=None):
        deps = {}

        def add(tok):
            if tok is None:
                return
            s, v = tok[0], tok[1]
            if deps.get(s, 0) < v:
                deps[s] = v

        for c in r:
            add(self._st(c)["w"])
        for c in w:
            st = self._st(c)
            add(st["w"])
            for s, v in st["r"].items():
                add((s, v))
        waits = []
        for s, v in deps.items():
            if s is eng.sem and dsem is None and not eng.same_sync:
                continue
            if eng.known.get(s, 0) >= v:
                continue
            eng.known[s] = v
            waits.append((s, v))
        if dsem is None:
            eng.sem.count += 1
            tok = [eng.sem, eng.sem.count]
            inc = (eng.sem, 1)
        else:
            dsem.count += 16
            tok = [dsem, dsem.count]
            inc = (dsem, 16)
            if group is not None:
                group.append(tok)
        for c in r:
            st = self._st(c)
            if st["r"].get(tok[0], 0) < tok[1]:
                st["r"][tok[0]] = tok[1]
        for c in w:
            st = self._st(c)
            st["w"] = tok
            st["r"] = {}
        eng.items.append((waits, fn, inc))
        return tok

    @staticmethod
    def close_group(group, sem):
        for tok in group:
            tok[1] = sem.count


def replay(eng, e):
    for waits, fn, inc in eng.items:
        for s, v in waits:
            e.wait_ge(s.handle, v)
        ins = fn(e)
        ins.then_inc(inc[0].handle, inc[1])


def build_nc(S):
    NCH = S // T
    nc = bass.Bass("TRN2", target_bir_lowering=False)

    def din(name, shape, dt=F32):
        return nc.dram_tensor(name, shape, dt, kind="ExternalInput").ap()

    x_d = din("x", [S, D])
    pos_d = din("pos", [1, S], I32)
    ccol_d = din("ccol", [128, 8])
    adaw_d = din("ada_w", [D, 6 * D])
    adabc_d = din("adab_col", [128, 32])
    adabg_d = din("adab_g", [2, D])
    n1g_d = din("n1g", [128, 8])
    n2g_d = din("n2g", [128, 8])
    winl_d = din("w_in_l", [9, 128, 8 * 256])
    bcol_d = din("bcol", [128, 17])
    bv_d = din("bv", [1, 128])
    cw_d = din("cw", [128, 12])
    sinks_d = din("sinks", [1, 8])
    gattn_d = din("gattn", [1, 512])
    gconv_d = din("gconv", [128, 4])
    wo_d = din("w_o", [D, D])
    wupl_d = din("w_up_l", [NJ, 128, 8 * 256])
    fcw_d = din("fcw", [128, NJ * 3])
    fcb_d = din("fcb", [128, NJ])
    wdn_d = din("w_down", [DFF, D])
    gfin_d = din("gfin", [1, D])
    ident_d = din("ident", [128, 128])
    rperm_d = din("rperm", [128, 128])
    masks_d = din("masks", [128, 256])
    invf_d = din("invf", [128, 1])
    out_d = nc.dram_tensor("out", [S, D], F32, kind="ExternalOutput").ap()
    ws_bf = nc.dram_tensor("ws_bf", [9 + NJ, 128, 8 * 256], BF16, kind="Internal").ap()
    wo_bf = nc.dram_tensor("wo_bf", [D, D], BF16, kind="Internal").ap()
    wdn_bf = nc.dram_tensor("wdn_bf", [DFF, D], BF16, kind="Internal").ap()

    with ExitStack() as stack:
        def sb(name, shape, dt=F32):
            return stack.enter_context(nc.sbuf_tensor("sb_" + name, shape, dt))

        Sx = Sched(nc, stack)
        PE = Sx.neweng("pe", False)
        ACT = Sx.neweng("act", True)
        DVE = Sx.neweng("dve", True)
        POOL = Sx.neweng("pool", False)
        SP = Sx.neweng("sp", False)

        ps = [stack.enter_context(nc.psum_tensor(f"ps{i}", [128, 512], F32)) for i in range(8)]
        psb = [p[:].bitcast(BF16) for p in ps]

        xin = [sb(f"xin{i}", [128, 4, D]) for i in range(2)]
        xn = [sb(f"xn{i}", [128, D], BF16) for i in range(2)]
        hTm = sb("hTm", [128, 8, T], BF16)
        hTf = sb("hTf", [128, 8, T], BF16)
        wsi = [sb(f"wsi{i}", [128, 8, 256], BF16) for i in range(NWI)]
        wsu = [sb(f"wsu{i}", [128, 8, 256], BF16) for i in range(NWU)]
        wdn = [sb(f"wdn{i}", [128, 512], BF16) for i in range(NWD)]
        Wo_sb = sb("Wo_sb", [128, 8, D], BF16)
        qkraw = sb("qkraw", [128, 5, T], BF16)
        qr = sb("qr", [128, 4, T], BF16)
        kz = [[sb(f"kz{s}{g}", [128, T], BF16) for g in range(2)] for s in range(2)]
        posi = sb("posi", [128, T], I32)
        cosT = sb("cosT", [128, T])
        sinT = sb("sinT", [128, T])
        t1 = sb("t1", [128, T])
        t2 = sb("t2", [128, T])
        Vaug = [sb(f"Vaug{s}", [128, 4, 2, 65], BF16) for s in range(2)]
        maskb = [sb(f"maskb{i}", [128, T], BF16) for i in range(2)]
        pT = [sb(f"pT{i}", [128, T], BF16) for i in range(8)]
        yat = sb("yat", [128, 512])
        ynb = [sb(f"ynb{i}", [128, 512], BF16) for i in range(2)]
        xs_sb = [sb(f"xs_sb{i}", [128, T]) for i in range(2)]
        ub = [sb(f"ub{i}", [128, T + 2]) for i in range(2)]
        ucar = sb("ucar", [128, 4, 2])
        cbuf = [sb(f"cbuf{i}", [128, T]) for i in range(2)]
        yconv = sb("yconv", [128, 4, T])
        ysq = [sb(f"ysq{i}", [128, T], BF16) for i in range(2)]
        rstd_c = sb("rstd_c", [128, T])
        rc_v = sb("rc_v", [128, T])
        rc_t = sb("rc_t", [128, T])
        ang, kf, rr, absr = rc_v, rc_t, cbuf[0], cbuf[1]
        yT = sb("yT", [128, 8, T], BF16)
        a1b = [sb(f"a1b{i}", [128, T]) for i in range(2)]
        fT = sb("fT", [128, NJ, T], BF16)
        gcar = sb("gcar", [128, NJ, 2])
        stt_ = sb("stt_", [128, 64])
        rs_v = sb("rs_v", [128, 64])
        rs_y = sb("rs_y", [128, 64])
        rs_t = sb("rs_t", [128, 64])
        den = sb("den", [128, 8])
        rec = sb("rec", [128, 8])
        junk_m = cbuf[1][:].bitcast(BF16)
        junk_f = a1b[1][:].bitcast(BF16)
        ccol = sb("ccol", [128, 8])
        ca = sb("ca", [128, 8])
        ca_bf = sb("ca_bf", [128, 8], BF16)
        ca_rep = qr[:].rearrange("p a t -> p (a t)")[:, 0:1024].rearrange("p (k m) -> p k m", k=8)
        adabc = sb("adabc", [128, 32])
        modc = sb("modc", [128, 32])
        n1g = sb("n1g", [128, 8])
        n2g = sb("n2g", [128, 8])
        a1c = sb("a1c", [128, 8])
        a2c = sb("a2c", [128, 8])
        fT32 = fT[:].rearrange("p j t -> p (j t)").bitcast(F32)
        g_b = [fT32[:, 0:1024], fT32[:, 1024:2048]]
        g_cells = [[("fT", j) for j in range(0, 4)], [("fT", j) for j in range(4, 8)]]
        wd_stg = [fT32[:, 2048:3072], fT32[:, 3072:4096]]
        wd_stg_cells = [[("fT", j) for j in range(8, 12)], [("fT", j) for j in range(12, 16)]]
        wd_ob = [fT32[:, 4096:4608].bitcast(BF16), fT32[:, 4608:5120].bitcast(BF16)]
        wd_ob_cells = [[("fT", 16), ("fT", 17)], [("fT", 18), ("fT", 19)]]
        wo_stg = yT[:].rearrange("p k t -> p (k t)").bitcast(F32).rearrange("p (k n) -> p k n", k=2)
        yT_all = [("yT", k, tt) for k in range(8) for tt in range(4)]
        bcol = sb("bcol", [128, 17])
        bv_b = sb("bv_b", [128, 128])
        cw = sb("cw", [128, 12])
        sinks_b = sb("sinks_b", [128, 8])
        esink = sb("esink", [128, 8])
        gattn_b = sb("gattn_b", [128, 512])
        gconv = sb("gconv", [128, 4])
        fcw = sb("fcw", [128, NJ * 3])
        fcb = sb("fcb", [128, NJ])
        gfin_b = sb("gfin_b", [128, D])
        cst_f = a1b[0]
        ident_b = sb("ident_b", [128, 128], BF16)
        rperm_b = sb("rperm_b", [128, 128], BF16)
        ones_b = sb("ones_b", [128, 128], BF16)
        invf = sb("invf", [128, 1])
        halfpi = sb("halfpi", [128, 1])

        def mm(out, lhsT, rhs, start, stop, r, w):
            Sx.op(PE, lambda e: e.matmul(out, lhsT=lhsT, rhs=rhs, start=start, stop=stop), r, w)

        def tr(out, in_, r, w):
            Sx.op(PE, lambda e: e.transpose(out, in_, ident_b[:]), list(r) + [("ident_b",)], w)

        def act(out, in_, func, r, w, bias=None, scale=None, accum=None):
            kw = {}
            if bias is not None:
                kw["bias"] = bias
            if scale is not None:
                kw["scale"] = scale
            if accum is not None:
                kw["accum_out"] = accum
            Sx.op(ACT, lambda e: e.activation(out=out, in_=in_, func=func, **kw), r, w)

        def vtt(out, in0, in1, op, r, w, eng=None):
            Sx.op(eng or DVE, lambda e: e.tensor_tensor(out=out, in0=in0, in1=in1, op=op), r, w)

        def vts(out, in0, s1, s2, op0, op1, r, w):
            if op1 is None:
                Sx.op(DVE, lambda e: e.tensor_scalar(out=out, in0=in0, scalar1=s1, scalar2=None, op0=op0), r, w)
            else:
                Sx.op(DVE, lambda e: e.tensor_scalar(out=out, in0=in0, scalar1=s1, scalar2=s2, op0=op0, op1=op1), r, w)

        def vstt(out, in0, scalar, in1, op0, op1, r, w):
            Sx.op(DVE, lambda e: e.scalar_tensor_tensor(out=out, in0=in0, scalar=scalar, in1=in1, op0=op0, op1=op1), r, w)

        def vcopy(out, in_, r, w):
            Sx.op(DVE, lambda e: e.tensor_copy(out=out, in_=in_), r, w)

        def vmemset(ap, val, w):
            Sx.op(DVE, lambda e: e.memset(ap, val), (), w)

        def dma(q, out, in_, sem, r, w, group=None):
            return Sx.op(q, lambda e: e.dma_start(out=out, in_=in_), r, w, dsem=sem, group=group)

        def rsqrt_col(cells, vcol, ycol, tcol, iters=2, single=False):
            rw = list(cells)
            vts(ycol.bitcast(I32), vcol.bitcast(I32), -0.5, 1597463007.0, ALU.mult, ALU.add, rw, rw)
            for _ in range(iters):
                if single:
                    vstt(tcol, ycol, vcol, ycol, ALU.mult, ALU.mult, rw, rw)
                else:
                    vtt(tcol, ycol, ycol, ALU.mult, rw, rw)
                    vtt(tcol, tcol, vcol, ALU.mult, rw, rw)
                vts(tcol, tcol, -0.5, 1.5, ALU.mult, ALU.add, rw, rw)
                vtt(ycol, ycol, tcol, ALU.mult, rw, rw)

        sem_small = Sx.newsem("d_small")
        grp = []

        def lsmall(t_ap, d_ap, cell):
            dma(SP, t_ap, d_ap, sem_small, [], cell if isinstance(cell, list) else [cell], group=grp)

        lsmall(ccol[:], ccol_d[:, :], ("ccol",))
        lsmall(adabc[:], adabc_d[:, :], ("adabc",))
        lsmall(n1g[:], n1g_d[:, :], ("n1g",))
        lsmall(n2g[:], n2g_d[:, :], ("n2g",))
        lsmall(bcol[:], bcol_d[:, :], ("bcol",))
        lsmall(bv_b[:], bv_d[0:1, :].partition_broadcast(128), ("bv_b",))
        lsmall(cw[:], cw_d[:, :], ("cw",))
        lsmall(sinks_b[:], sinks_d[0:1, :].partition_broadcast(128), ("sinks_b",))
        lsmall(gattn_b[:], gattn_d[0:1, :].partition_broadcast(128), ("gattn_b",))
        lsmall(gconv[:], gconv_d[:, :], ("gconv",))
        lsmall(fcw[:], fcw_d[:, :], ("fcw",))
        lsmall(fcb[:], fcb_d[:, :], ("fcb",))
        lsmall(gfin_b[:], gfin_d[0:1, :].partition_broadcast(128), ("gfin_b",))
        lsmall(cst_f[:, 0:128], ident_d[:, :], ("a1b", 0))
        lsmall(cst_f[:, 128:256], rperm_d[:, :], ("a1b", 0))
        lsmall(cst_f[:, 256:512], masks_d[:, :], ("a1b", 0))
        lsmall(invf[:], invf_d[:, :], ("invf",))
        lsmall(g_b[0], adabg_d[0:1, :].partition_broadcast(128), g_cells[0])
        lsmall(g_b[1], adabg_d[1:2, :].partition_broadcast(128), g_cells[1])
        Sched.close_group(grp, sem_small)

        cast_n = {"i": 0}

        def new_cast_sem():
            cast_n["i"] += 1
            return Sx.newsem(f"d_cast{cast_n['i']}")

        sem_ada = [Sx.newsem(f"d_ada{i}") for i in range(2)]
        xin_cells = lambda s: [("xin", s, tt, h) for tt in range(4) for h in range(2)]
        adasb = [xin[i][:].rearrange("p a b -> p (a b)").bitcast(BF16).rearrange("p (k n) -> p k n", k=8)
                 for i in range(2)]

        def ada_piece(v):
            s = v % 2
            dma(POOL, adasb[s], adaw_d[:, v * D:(v + 1) * D].rearrange("(k p) n -> p k n", p=128),
                sem_ada[s], [], xin_cells(s))

        vcopy(ident_b[:], cst_f[:, 0:128], [("a1b", 0)], [("ident_b",)])
        vcopy(rperm_b[:], cst_f[:, 128:256], [("a1b", 0)], [("rperm_b",)])
        for i_ in range(2):
            vts(maskb[i_][:].rearrange("p (h q) -> p h q", h=4),
                cst_f[:, 256 + i_ * 128:256 + (i_ + 1) * 128].unsqueeze(1).broadcast_to([128, 4, 128]),
                -1.0, 30000.0, ALU.add, ALU.mult, [("a1b", 0)], [("maskb",)])
        vmemset(ones_b[:], 1.0, [("ones_b",)])
        vmemset(halfpi[:], math.pi / 2, [("halfpi",)])
        vmemset(ucar[:].rearrange("p a b -> p (a b)"), 0.0, [("ucar", cc) for cc in range(4)])
        vmemset(gcar[:].rearrange("p a b -> p (a b)"), 0.0, [("gcar", j) for j in range(NJ)])
        for s in range(2):
            vmemset(Vaug[s][:].rearrange("p a b c -> p (a b c)"), 1.0, [("Vaug", s, tt) for tt in range(4)])
            for g in range(2):
                vmemset(kz[s][g][:], 0.0, [("kz", s, g)])
        act(ca[:], ccol[:], AF.Silu, [("ccol",)], [("ca",)])
        act(esink[:], sinks_b[:], AF.Exp, [("sinks_b",)], [("esink",)])
        vcopy(ca_bf[:], ca[:], [("ca",)], [("ca_bf",)])
        qr_all = [("qr", a_, q_) for a_ in range(4) for q_ in range(4)]
        vcopy(ca_rep, ca[:, :].unsqueeze(2).broadcast_to([128, 8, 128]), [("ca",)], qr_all)

        for i0 in range(0, 9, 3):
            dma(POOL, ws_bf[i0:i0 + 3], winl_d[i0:i0 + 3], new_cast_sem(), [], [("ws", i) for i in range(i0, i0 + 3)])
        ada_piece(0)
        ada_piece(1)
        colv = {0: 0, 1: 1, 3: 2, 4: 3}
        for v in range(6):
            s = v % 2
            if v in colv:
                for oc in range(8):
                    col = colv[v] * 8 + oc
                    for kc in range(8):
                        mm(ps[0][:, col:col + 1], adasb[s][:, kc, oc * 128:(oc + 1) * 128], ca_bf[:, kc:kc + 1],
                           kc == 0, kc == 7, xin_cells(s) + [("ca_bf",)], [("ps", 0)])
            else:
                gi = 0 if v == 2 else 1
                for h in range(2):
                    for kc in range(8):
                        mm(ps[1 + h][:, :], ca_rep[:, kc, :], adasb[s][:, kc, h * 512:(h + 1) * 512],
                           kc == 0, kc == 7, xin_cells(s) + qr_all, [("ps", 1 + h)])
                    vtt(g_b[gi][:, h * 512:(h + 1) * 512], ps[1 + h][:, :], g_b[gi][:, h * 512:(h + 1) * 512],
                        ALU.add, [("ps", 1 + h)] + g_cells[gi], g_cells[gi])
            if v + 2 < 6:
                ada_piece(v + 2)
        vtt(modc[:], ps[0][:, 0:32], adabc[:], ALU.add, [("ps", 0), ("adabc",)], [("modc",)])
        vts(a1c[:], modc[:, 8:16], 1.0, None, ALU.add, None, [("modc",)], [("a1c",)])
        vtt(a1c[:], a1c[:], n1g[:], ALU.mult, [("a1c",), ("n1g",)], [("a1c",)])
        vts(a2c[:], modc[:, 24:32], 1.0, None, ALU.add, None, [("modc",)], [("a2c",)])
        vtt(a2c[:], a2c[:], n2g[:], ALU.mult, [("a2c",), ("n2g",)], [("a2c",)])
        sh1 = modc[:, 0:8]
        sh2 = modc[:, 16:24]

        for j0 in range(0, NJ, 4):
            j1 = min(NJ, j0 + 4)
            dma(POOL, ws_bf[9 + j0:9 + j1], wupl_d[j0:j1], new_cast_sem(), [], [("ws", 9 + j) for j in range(j0, j1)])

        sem_wo = Sx.newsem("d_wo")
        for q4 in range(4):
            dma(SP, wo_stg, wo_d[q4 * 256:(q4 + 1) * 256, :].rearrange("(k p) n -> p k n", p=128), sem_wo, [], yT_all)
            vtt(Wo_sb[:, 2 * q4:2 * q4 + 2, :], wo_stg, g_b[0].unsqueeze(1).broadcast_to([128, 2, D]), ALU.mult,
                yT_all + g_cells[0], [("Wo_sb",)])
        sem_wds = [Sx.newsem(f"d_wds{i}") for i in range(2)]
        sem_wdo = [Sx.newsem(f"d_wdo{i}") for i in range(2)]
        for j in range(NJ):
            s = j % 2
            dma(SP, wd_stg[s], wdn_d[j * 128:(j + 1) * 128, :], sem_wds[s], [], wd_stg_cells[s])
            vtt(wd_ob[s], wd_stg[s], g_b[1], ALU.mult, wd_stg_cells[s] + g_cells[1], wd_ob_cells[s])
            dma(SP, wdn_bf[j * 128:(j + 1) * 128, :], wd_ob[s], sem_wdo[s], wd_ob_cells[s], [("wdnbf", j)])

        sem_x = [[Sx.newsem(f"d_x{i}{h}") for h in range(2)] for i in range(2)]
        sem_pos = Sx.newsem("d_pos")
        sem_out = [[Sx.newsem(f"d_out{i}{h}") for h in range(2)] for i in range(2)]
        sem_wi = [Sx.newsem(f"d_wi{i}") for i in range(NWI)]
        sem_wu = [Sx.newsem(f"d_wu{i}") for i in range(NWU)]
        sem_wd = [Sx.newsem(f"d_wd{i}") for i in range(NWD)]

        def load_x(c, half):
            s = c % 2
            t0 = c * T + half * 256
            dma(POOL, xin[s][:, half * 2:half * 2 + 2, :], x_d[t0:t0 + 256, :].rearrange("(t p) d -> p t d", p=128),
                sem_x[s][half], [], [("xin", s, tt, h) for tt in (half * 2, half * 2 + 1) for h in range(2)])

        def load_pos(c):
            dma(POOL, posi[:], pos_d[0:1, c * T:(c + 1) * T].partition_broadcast(128), sem_pos, [], [("posi",)])

        def make_ring(seq, bufs, sems, cellname, src):
            st = {"issued": 0, "used": 0}
            n = len(bufs)

            def ensure(upto):
                while st["issued"] < min(len(seq), upto):
                    i = st["issued"]
                    s = i % n
                    src(seq[i], bufs[s], sems[s], (cellname, s))
                    st["issued"] += 1

            def nxt():
                i = st["used"]
                ensure(i + n)
                st["used"] += 1
                return i % n

            return nxt

        wi_next = make_ring([p for c in range(NCH) for p in range(9)], wsi, sem_wi, "wsi",
                            lambda p, buf, sem, cell: dma(SP, buf[:], ws_bf[p].rearrange("p (k n) -> p k n", k=8), sem,
                                                          [("ws", p)], [cell]))
        wu_next = make_ring([9 + j for c in range(NCH) for j in range(NJ)], wsu, sem_wu, "wsu",
                            lambda p, buf, sem, cell: dma(SP, buf[:], ws_bf[p].rearrange("p (k n) -> p k n", k=8), sem,
                                                          [("ws", p)], [cell]))
        wd_next = make_ring([(j, ph) for c in range(NCH) for ph in range(2) for j in range(NJ)], wdn, sem_wd, "wdn",
                            lambda jp, buf, sem, cell: dma(SP, buf[:], wdn_bf[jp[0] * 128:(jp[0] + 1) * 128,
                                                                              jp[1] * 512:(jp[1] + 1) * 512], sem,
                                                           [("wdnbf", jp[0])], [cell]))

        rp = {"i": 0, "banks": [5, 6]}

        def rp_next():
            b = rp["banks"][rp["i"] % len(rp["banks"])]
            rp["i"] += 1
            return b

        stc = {"i": 0}

        def st_block(n):
            if stc["i"] % 64 + n > 64:
                stc["i"] += 64 - stc["i"] % 64
            i = stc["i"] % 64
            stc["i"] += n
            return i

        def rstd_block(col, n, n_inv):
            cells = [("rs", col + i) for i in range(n)]
            vts(rs_v[:, col:col + n], stt_[:, col:col + n], n_inv, EPS, ALU.mult, ALU.add,
                [("st", col + i) for i in range(n)], cells)
            rsqrt_col(cells, rs_v[:, col:col + n], rs_y[:, col:col + n], rs_t[:, col:col + n], single=False)

        def staged(n, stages, cost):
            ns = len(stages)
            for t in range(n + ns - 1):
                for s_ in range(ns - 1, -1, -1):
                    k = t - s_
                    if 0 <= k < n:
                        stages[s_](k)
                yield cost

        def norm_to_hT(slot, hT, hname, ac, sh, ac_cell, sh_cell, banks, junk, junk_cell):
            col = st_block(4)
            for tt in range(4):
                xc = [("xin", slot, tt, 0), ("xin", slot, tt, 1)]
                act(junk, xin[slot][:, tt, :], AF.Square, xc, [junk_cell, ("st", col + tt)], accum=stt_[:, col + tt:col + tt + 1])
            yield 2.0
            rstd_block(col, 4, 1.0 / D)
            yield 2.0

            def s_xn(tt):
                xc = [("xin", slot, tt, 0), ("xin", slot, tt, 1)]
                vts(xn[tt % 2][:], xin[slot][:, tt, :], rs_y[:, col + tt:col + tt + 1], None, ALU.mult, None,
                    xc + [("rs", col + tt)], [("xn", tt % 2)])

            def s_tr(tt):
                bank = banks[tt % len(banks)]
                for kc in range(8):
                    tr(psb[bank][:, kc * 128:(kc + 1) * 128], xn[tt % 2][:, kc * 128:(kc + 1) * 128], [("xn", tt % 2)],
                       [("ps", bank)])

            def s_ev(tt):
                bank = banks[tt % len(banks)]
                for kc in range(8):
                    if kc % 8 < 8:
                        act(hT[:, kc, tt * 128:(tt + 1) * 128], psb[bank][:, kc * 128:(kc + 1) * 128], AF.Identity,
                            [("ps", bank), ac_cell, sh_cell], [(hname, kc, tt)], bias=sh[:, kc:kc + 1], scale=ac[:, kc:kc + 1])
                    else:
                        vts(hT[:, kc, tt * 128:(tt + 1) * 128], psb[bank][:, kc * 128:(kc + 1) * 128], ac[:, kc:kc + 1],
                            sh[:, kc:kc + 1], ALU.mult, ALU.add, [("ps", bank), ac_cell, sh_cell], [(hname, kc, tt)])

            yield from staged(4, [s_xn, s_tr, s_ev], 2.5)

        def mixer(c):
            slot = c % 2
            sK = c % 2
            vcopy(kf[:], posi[:], [("posi",)], [("rc",)])
            vts(ang[:], kf[:], invf[:, 0:1], None, ALU.mult, None, [("rc",), ("invf",)], [("rc",)])
            vts(t1[:], ang[:], 1.0 / (2 * math.pi), None, ALU.mult, None, [("rc",)], [("t1",)])
            vcopy(posi[:], t1[:], [("t1",)], [("posi",)])
            vcopy(kf[:], posi[:], [("posi",)], [("rc",)])
            yield 2.0
            vstt(rr[:], kf[:], -C1, ang[:], ALU.mult, ALU.add, [("rc",)], [("cbuf", 0)])
            vstt(rr[:], kf[:], -C2, rr[:], ALU.mult, ALU.add, [("rc",), ("cbuf", 0)], [("cbuf", 0)])
            vts(rr[:], rr[:], 3.1415925, -3.1415925, ALU.min, ALU.max, [("cbuf", 0)], [("cbuf", 0)])
            yield 2.0
            act(sinT[:], rr[:], AF.Sin, [("cbuf", 0)], [("sinT",)])
            act(absr[:], rr[:], AF.Abs, [("cbuf", 0)], [("cbuf", 1)])
            act(cosT[:], absr[:], AF.Sin, [("cbuf", 1), ("halfpi",)], [("cosT",)], bias=halfpi[:, 0:1], scale=-1.0)
            if c + 1 < NCH:
                load_pos(c + 1)
            yield 1.0
            yield from norm_to_hT(slot, hTm, "hTm", a1c, sh1, ("a1c",), ("modc",), [4, 5, 6, 7], junk_m, ("cbuf", 1))

            rp["banks"] = [5, 6, 4]
            info = {}

            def win_mm(a):
                if a < 17:
                    if a % 2 == 0:
                        info["cur"] = wi_next()
                    cur = info["cur"]
                    sub = a % 2
                    bank = rp_next()
                    info[a] = bank
                    for kc in range(8):
                        mm(ps[bank][:, :], wsi[cur][:, kc, sub * 128:(sub + 1) * 128], hTm[:, kc, :], kc == 0, kc == 7,
                           [("wsi", cur)] + [("hTm", kc, tt) for tt in range(4)], [("ps", bank)])
                else:
                    tt = a - 17
                    cur = info["cur"]
                    bank = rp_next()
                    info[a] = bank
                    for kc in range(8):
                        mm(ps[bank][:, 0:128], hTm[:, kc, tt * 128:(tt + 1) * 128], wsi[cur][:, kc, 128:256], kc == 0, kc == 7,
                           [("wsi", cur), ("hTm", kc, tt)], [("ps", bank)])

            def win_ev(a):
                bank = info[a]
                if a >= 17:
                    tt = a - 17
                    vtt(Vaug[sK][:, tt, :, 0:64], ps[bank][:, 0:128].rearrange("p (g d) -> p g d", g=2),
                        bv_b[:].rearrange("p (g d) -> p g d", g=2), ALU.add, [("ps", bank), ("bv_b",)], [("Vaug", sK, tt)])
                    return
                kind = A_KIND[a]
                bc = bcol[:, a:a + 1]
                if kind in ("q", "k"):
                    act(qkraw[:, a, :], ps[bank][:, :], AF.Identity, [("ps", bank), ("bcol",)], [("qkraw", a)], bias=bc)
                    return
                cc = (a - 5) // 3
                s2 = cc % 2
                if kind == "xs":
                    act(xs_sb[s2][:], ps[bank][:, :], AF.Identity, [("ps", bank), ("bcol",)], [("xs_sb", s2)], bias=bc)
                elif kind == "gb":
                    vcopy(ub[s2][:, 0:2], ucar[:, cc, :], [("ucar", cc)], [("ub", s2)])
                    vstt(ub[s2][:, 2:T + 2], ps[bank][:, :], bc, xs_sb[s2][:], ALU.add, ALU.mult,
                         [("ps", bank), ("bcol",), ("xs_sb", s2)], [("ub", s2)])
                    vcopy(ucar[:, cc, :], ub[s2][:, T:T + 2], [("ub", s2)], [("ucar", cc)])
                    act(cbuf[s2][:], ub[s2][:, 2:T + 2], AF.Copy, [("ub", s2), ("cw",)], [("cbuf", s2)],
                        scale=cw[:, cc * 3 + 2:cc * 3 + 3])
                    vstt(cbuf[s2][:], ub[s2][:, 1:T + 1], cw[:, cc * 3 + 1:cc * 3 + 2], cbuf[s2][:], ALU.mult, ALU.add,
                         [("ub", s2), ("cw",), ("cbuf", s2)], [("cbuf", s2)])
                    vstt(cbuf[s2][:], ub[s2][:, 0:T], cw[:, cc * 3:cc * 3 + 1], cbuf[s2][:], ALU.mult, ALU.add,
                         [("ub", s2), ("cw",), ("cbuf", s2)], [("cbuf", s2)])
                else:
                    vstt(yconv[:, cc, :], ps[bank][:, :], bc, cbuf[s2][:], ALU.add, ALU.mult,
                         [("ps", bank), ("bcol",), ("cbuf", s2)], [("yconv", cc)])
                    act(ysq[s2][:], yconv[:, cc, :], AF.Square, [("yconv", cc)], [("ysq", s2)])

            def win_ss(a):
                if a < 17 and A_KIND[a] == "gc":
                    cc = (a - 5) // 3
                    s2 = cc % 2
                    mm(ps[7][:, :], ones_b[:], ysq[s2][:], cc == 0, cc == 3, [("ones_b",), ("ysq", s2)], [("ps", 7)])

            yield from staged(21, [win_mm, win_ev, win_ss], 2.2)

            def rope_mm(a):
                bank = rp_next()
                info[("r", a)] = bank
                mm(ps[bank][:, :], rperm_b[:], qkraw[:, a, :], True, True, [("rperm_b",), ("qkraw", a)], [("ps", bank)])

            def rope_ev(a):
                bank = info[("r", a)]
                vtt(t1[:], qkraw[:, a, :], cosT[:], ALU.mult, [("qkraw", a), ("cosT",)], [("t1",)])
                vtt(t2[:], ps[bank][:, :], sinT[:], ALU.mult, [("ps", bank), ("sinT",)], [("t2",)])
                if a < 4:
                    vtt(qr[:, a, :], t1[:], t2[:], ALU.add, [("t1",), ("t2",)], [("qr", a, qb) for qb in range(4)])
                else:
                    vtt(kz[sK][0][0:64, :], t1[0:64, :], t2[0:64, :], ALU.add, [("t1",), ("t2",)], [("kz", sK, 0)])
                    vtt(kz[sK][1][64:128, :], t1[64:128, :], t2[64:128, :], ALU.add, [("t1",), ("t2",)], [("kz", sK, 1)])

            yield from staged(5, [rope_mm, rope_ev], 1.8)

            vts(rc_v[:], ps[7][:, :], 1.0 / 512, EPS, ALU.mult, ALU.add, [("ps", 7)], [("rc",)])
            rw = [("rc",)]
            vts(rstd_c[:].bitcast(I32), rc_v[:].bitcast(I32), -0.5, 1597463007.0, ALU.mult, ALU.add, rw, rw)
            yield 1.2
            for _ in range(2):
                vtt(rc_t[:], rstd_c[:], rstd_c[:], ALU.mult, rw, rw)
                vtt(rc_t[:], rc_t[:], rc_v[:], ALU.mult, rw, rw)
                yield 1.2
                vts(rc_t[:], rc_t[:], -0.5, 1.5, ALU.mult, ALU.add, rw, rw)
                vtt(rstd_c[:], rstd_c[:], rc_t[:], ALU.mult, rw, rw)
                yield 1.2
            for cc in range(4):
                vstt(yT[:, 4 + cc, :], yconv[:, cc, :], gconv[:, cc:cc + 1], rstd_c[:], ALU.mult, ALU.mult,
                     [("yconv", cc), ("gconv",), ("rc",)], [("yT", 4 + cc, tt) for tt in range(4)])
                if cc % 2 == 1:
                    yield 1.2

            rp["banks"] = [5, 6]
            OB = [7, 4]
            att = {}

            def kv_src(qb, kb):
                if kb == "cur":
                    return sK, qb
                if qb > 0:
                    return sK, qb - 1
                return 1 - sK, 3

            def a_scores(qb):
                n = c * 4 + qb
                kbs = ([] if n == 0 else ["prev"]) + ["cur"]
                tiles = []
                for g in range(2):
                    for kb in kbs:
                        ksl, kcol = kv_src(qb, kb)
                        bank = rp_next()
                        mm(ps[bank][:, :], kz[ksl][g][:, kcol * 128:(kcol + 1) * 128],
                           qr[:, :, qb * 128:(qb + 1) * 128], True, False,
                           [("kz", ksl, g)] + [("qr", a, qb) for a in range(4)], [("ps", bank)])
                        mb = maskb[0] if kb == "cur" else maskb[1]
                        mm(ps[bank][:, :], ident_b[:], mb[:], False, True, [("ident_b",), ("maskb",)], [("ps", bank)])
                        ti = (qb * 4 + len(tiles)) % 8
                        act(pT[ti][:], ps[bank][:, :], AF.Exp, [("ps", bank)], [("pT", ti)], scale=0.125)
                        tiles.append((g, kb, ti))
                att[qb] = (kbs, tiles)

            def a_pv(qb):
                kbs, tiles = att[qb]
                for g in range(2):
                    po = ps[OB[g]][:, 0:260].rearrange("p (h d) -> p h d", h=4)
                    for i in range(4):
                        for ki, kb in enumerate(kbs):
                            vsl, vt = kv_src(qb, kb)
                            ti = [t_ for (g_, kb_, t_) in tiles if g_ == g and kb_ == kb][0]
                            mm(po[:, i, :], pT[ti][:, i * 128:(i + 1) * 128], Vaug[vsl][:, vt, g, :], ki == 0,
                               ki == len(kbs) - 1, [("pT", ti), ("Vaug", vsl, vt)], [("ps", OB[g])])

            def a_norm(qb):
                for g in range(2):
                    po = ps[OB[g]][:, 0:260].rearrange("p (h d) -> p h d", h=4)
                    vtt(den[:, g * 4:(g + 1) * 4], po[:, :, 64], esink[:, g * 4:(g + 1) * 4], ALU.add,
                        [("ps", OB[g]), ("esink",)], [("den",)])
                Sx.op(DVE, lambda e: e.reciprocal(out=rec[:], in_=den[:]), [("den",)], [("rec",)])
                for g in range(2):
                    po = ps[OB[g]][:, 0:260].rearrange("p (h d) -> p h d", h=4)
                    vtt(yat[:, g * 256:(g + 1) * 256].rearrange("p (h d) -> p h d", h=4), po[:, :, 0:64],
                        rec[:, g * 4:(g + 1) * 4].unsqueeze(2).broadcast_to([128, 4, 64]), ALU.mult,
                        [("ps", OB[g]), ("rec",)], [("yat",)])
                col = st_block(1)
                att[("col", qb)] = col
                act(junk_m[:, 0:512], yat[:], AF.Square, [("yat",)], [("cbuf", 1), ("st", col)], accum=stt_[:, col:col + 1])

            def a_yn(qb):
                col = att[("col", qb)]
                rstd_block(col, 1, 1.0 / 512)
                vstt(ynb[qb % 2][:], yat[:], rs_y[:, col:col + 1], gattn_b[:], ALU.mult, ALU.mult,
                     [("yat",), ("rs", col), ("gattn_b",)], [("ynb", qb % 2)])

            def a_tr(qb):
                bank = rp_next()
                att[("tb", qb)] = bank
                for j in range(4):
                    tr(psb[bank][:, j * 128:(j + 1) * 128], ynb[qb % 2][:, j * 128:(j + 1) * 128], [("ynb", qb % 2)],
                       [("ps", bank)])

            def a_ev(qb):
                bank = att[("tb", qb)]
                act(yT[:, 0:4, qb * 128:(qb + 1) * 128], psb[bank][:, 0:512].rearrange("p (k t) -> p k t", k=4), AF.Copy,
                    [("ps", bank)], [("yT", k, qb) for k in range(4)])

            yield from staged(4, [a_scores, a_pv, a_norm, a_yn, a_tr, a_ev], 2.2)

            def wo_mm(i):
                tt, h = divmod(i, 2)
                bank = rp_next()
                info[("o", i)] = bank
                for kc in range(8):
                    mm(ps[bank][:, :], yT[:, kc, tt * 128:(tt + 1) * 128], Wo_sb[:, kc, h * 512:(h + 1) * 512], kc == 0,
                       kc == 7, [("yT", kc, tt), ("Wo_sb",)], [("ps", bank)])

            def wo_ev(i):
                tt, h = divmod(i, 2)
                bank = info[("o", i)]
                vtt(xin[slot][:, tt, h * 512:(h + 1) * 512], xin[slot][:, tt, h * 512:(h + 1) * 512], ps[bank][:, :], ALU.add,
                    [("ps", bank), ("xin", slot, tt, h)], [("xin", slot, tt, h)])

            yield from staged(8, [wo_mm, wo_ev], 2.2)

            yield from norm_to_hT(slot, hTf, "hTf", a2c, sh2, ("a2c",), ("modc",), [4, 7, 5, 6], junk_m, ("cbuf", 1))

        def ffn(c):
            slot = c % 2
            fi = {}

            def up_mm(j):
                ws_ = wu_next()
                bg, bvv = (0, 1) if j % 2 == 0 else (2, 3)
                for (bk, off) in ((bg, 0), (bvv, 128)):
                    for kc in range(8):
                        mm(ps[bk][:, :], wsu[ws_][:, kc, off:off + 128], hTf[:, kc, :], kc == 0, kc == 7,
                           [("wsu", ws_)] + [("hTf", kc, tt) for tt in range(4)], [("ps", bk)])

            def up_e1(j):
                bg, bvv = (0, 1) if j % 2 == 0 else (2, 3)
                s2 = j % 2
                w0 = fcw[:, j * 3:j * 3 + 1]
                w1 = fcw[:, j * 3 + 1:j * 3 + 2]
                w2 = fcw[:, j * 3 + 2:j * 3 + 3]
                act(a1b[s2][:], ps[bg][:, :], AF.Identity, [("ps", bg), ("fcw",), ("fcb",)], [("a1b", s2)],
                    bias=fcb[:, j:j + 1], scale=w2)
                vstt(a1b[s2][:, 1:T], ps[bg][:, 0:T - 1], w1, a1b[s2][:, 1:T], ALU.mult, ALU.add,
                     [("ps", bg), ("fcw",), ("a1b", s2)], [("a1b", s2)])
                vstt(a1b[s2][:, 2:T], ps[bg][:, 0:T - 2], w0, a1b[s2][:, 2:T], ALU.mult, ALU.add,
                     [("ps", bg), ("fcw",), ("a1b", s2)], [("a1b", s2)])
                vstt(a1b[s2][:, 0:2], gcar[:, j, 0:2], w0, a1b[s2][:, 0:2], ALU.mult, ALU.add,
                     [("gcar", j), ("fcw",), ("a1b", s2)], [("a1b", s2)])
                vstt(a1b[s2][:, 0:1], gcar[:, j, 1:2], w1, a1b[s2][:, 0:1], ALU.mult, ALU.add,
                     [("gcar", j), ("fcw",), ("a1b", s2)], [("a1b", s2)])
                vcopy(gcar[:, j, :], ps[bg][:, T - 2:T], [("ps", bg)], [("gcar", j)])

            def up_e2(j):
                bg, bvv = (0, 1) if j % 2 == 0 else (2, 3)
                s2 = j % 2
                act(a1b[s2][:], a1b[s2][:], AF.Silu, [("a1b", s2)], [("a1b", s2)])
                vtt(fT[:, j, :], a1b[s2][:], ps[bvv][:, :], ALU.mult, [("a1b", s2), ("ps", bvv)], [("fT", j)])

            yield from staged(NJ, [up_mm, up_e1, up_e2], 4.3)
            for ph in range(2):
                for j in range(NJ):
                    wd_ = wd_next()
                    for tt in range(4):
                        mm(ps[tt][:, :], fT[:, j, tt * 128:(tt + 1) * 128], wdn[wd_][:, :], j == 0, j == NJ - 1,
                           [("fT", j), ("wdn", wd_)], [("ps", tt)])
                    yield 1.1
                for tt in range(4):
                    vtt(xin[slot][:, tt, ph * 512:(ph + 1) * 512], xin[slot][:, tt, ph * 512:(ph + 1) * 512], ps[tt][:, :],
                        ALU.add, [("ps", tt), ("xin", slot, tt, ph)], [("xin", slot, tt, ph)])
                    if tt % 2 == 1:
                        yield 1.0
            col = st_block(4)
            for tt in range(4):
                xc = [("xin", slot, tt, 0), ("xin", slot, tt, 1)]
                act(junk_f, xin[slot][:, tt, :], AF.Square, xc, [("a1b", 1), ("st", col + tt)],
                    accum=stt_[:, col + tt:col + tt + 1])
            yield 1.0
            rstd_block(col, 4, 1.0 / D)
            yield 1.0
            for hf in range(2):
                for tt in (hf * 2, hf * 2 + 1):
                    xc = [("xin", slot, tt, 0), ("xin", slot, tt, 1)]
                    vstt(xin[slot][:, tt, :], xin[slot][:, tt, :], rs_y[:, col + tt:col + tt + 1], gfin_b[:], ALU.mult, ALU.mult,
                         xc + [("rs", col + tt), ("gfin_b",)], xc)
                dma(POOL, out_d[c * T + hf * 256:c * T + (hf + 1) * 256, :].rearrange("(t p) d -> p t d", p=128),
                    xin[slot][:, hf * 2:(hf + 1) * 2, :], sem_out[slot][hf],
                    [("xin", slot, tt, h) for tt in (hf * 2, hf * 2 + 1) for h in range(2)], [("out", c, hf)])
                if c + 2 < NCH:
                    load_x(c + 2, hf)
                yield 1.0

        def drain(gen):
            for _ in gen:
                pass

        def interleave(ga, ta, gb, tb, lead=0.0):
            pa = 0.0
            pb = -lead * tb
            da = db = False
            while not (da and db):
                if not da and (db or pa / ta <= pb / tb):
                    try:
                        pa += next(ga)
                    except StopIteration:
                        da = True
                else:
                    try:
                        pb += next(gb)
                    except StopIteration:
                        db = True

        for half in range(2):
            load_x(0, half)
        load_pos(0)
        if NCH > 1:
            for half in range(2):
                load_x(1, half)
        drain(mixer(0))
        for c in range(NCH):
            if c + 1 < NCH:
                interleave(ffn(c), 165.0, mixer(c + 1), 150.0, lead=0.08)
            else:
                drain(ffn(c))

        fin = [(s, s.count) for pair in sem_out for s in pair if s.count > 0]

        block = stack.enter_context(nc.Block())

        @block.tensor
        def _(e):
            replay(PE, e)

        @block.scalar
        def _(e):
            replay(ACT, e)

        @block.vector
        def _(e):
            replay(DVE, e)

        @block.gpsimd
        def _(e):
            replay(POOL, e)
            for s, v in fin:
                e.wait_ge(s.handle, v)

        @block.sync
        def _(e):
            replay(SP, e)

    return nc


def _host_consts():
    ident = np.eye(128, dtype=np.float32)
    rperm = np.zeros((128, 128), dtype=np.float32)
    for m in range(128):
        if (m % 64) < 32:
            rperm[m + 32, m] = -1.0
        else:
            rperm[m - 32, m] = 1.0
    kk = np.arange(128)[:, None]
    qq = np.arange(128)[None, :]
    mask_cur = (qq >= kk).astype(np.float32)
    mask_prev = (qq < kk).astype(np.float32)
    masks = np.concatenate([mask_cur, mask_prev], axis=1)
    inv = (np.float32(10000.0) ** (-(np.arange(32, dtype=np.float32)) / np.float32(32))).astype(np.float32)
    invf = np.tile(inv, 4)[:, None].astype(np.float32)
    return ident, rperm, masks, invf


def _col(v):
    v = np.asarray(v, dtype=np.float32)
    return np.ascontiguousarray(v.reshape(-1, 128).T)


def _prep_shared(ada_w, ada_b, norm1_g, w_in, b_in, conv_w, attn_sinks, out_norm_attn_g, out_norm_conv_g, w_o,
                 norm2_g, w_up, ffn_conv_w, ffn_conv_b, w_down, final_norm_g):
    f = np.float32
    w_in = np.asarray(w_in[0], f)
    b_in = np.asarray(b_in[0], f)
    q0, k0, v0, gb0, gc0, xs0 = 0, 512, 640, 768, 1280, 1792
    hperm = [0, 4, 1, 5, 2, 6, 3, 7]
    cols = []
    for a in range(4):
        for hh in (hperm[2 * a], hperm[2 * a + 1]):
            cols.extend(range(q0 + hh * 64, q0 + (hh + 1) * 64))
    cols.extend(range(k0, k0 + 128))
    for cc in range(4):
        cols.extend(range(xs0 + cc * 128, xs0 + (cc + 1) * 128))
        cols.extend(range(gb0 + cc * 128, gb0 + (cc + 1) * 128))
        cols.extend(range(gc0 + cc * 128, gc0 + (cc + 1) * 128))
    cols.extend(range(v0, v0 + 128))
    cols = np.array(cols)
    wl = w_in[:, cols]
    w_in_l = np.ascontiguousarray(wl.reshape(8, 128, 9, 256).transpose(2, 1, 0, 3)).reshape(9, 128, 2048)
    bl = b_in[cols]
    bcol = _col(bl[:17 * 128])
    bv = np.ascontiguousarray(bl[17 * 128:][None, :])
    w_up = np.asarray(w_up[0], f)
    gate = w_up[:, :DFF].reshape(8, 128, NJ, 128)
    val = w_up[:, DFF:].reshape(8, 128, NJ, 128)
    w_up_l = np.ascontiguousarray(np.concatenate([gate, val], axis=3).transpose(2, 1, 0, 3)).reshape(NJ, 128, 2048)
    ada_b = np.asarray(ada_b[0], f)
    adab_col = np.concatenate([_col(ada_b[v * D:(v + 1) * D]) for v in (0, 1, 3, 4)], axis=1)
    adab_g = np.ascontiguousarray(np.stack([ada_b[2 * D:3 * D], ada_b[5 * D:6 * D]]))
    cw = np.asarray(conv_w[0], f)
    cwl = np.ascontiguousarray(cw.reshape(3, 4, 128).transpose(2, 1, 0)).reshape(128, 12)
    fw = np.asarray(ffn_conv_w[0], f)
    fcw = np.ascontiguousarray(fw.reshape(3, NJ, 128).transpose(2, 1, 0)).reshape(128, NJ * 3)
    fcb = _col(np.asarray(ffn_conv_b[0], f))
    ident, rperm, masks, invf = _host_consts()
    return {
        "ada_w": np.ascontiguousarray(np.asarray(ada_w[0], f)),
        "adab_col": np.ascontiguousarray(adab_col),
        "adab_g": adab_g,
        "n1g": _col(norm1_g[0]),
        "n2g": _col(norm2_g[0]),
        "w_in_l": w_in_l,
        "bcol": bcol,
        "bv": bv,
        "cw": cwl,
        "sinks": np.ascontiguousarray(np.asarray(attn_sinks[0], f)[None, :]),
        "gattn": np.ascontiguousarray(np.asarray(out_norm_attn_g[0], f)[None, :]),
        "gconv": _col(out_norm_conv_g[0]),
        "w_o": np.ascontiguousarray(np.asarray(w_o[0], f)),
        "w_up_l": w_up_l,
        "fcw": fcw,
        "fcb": fcb,
        "w_down": np.ascontiguousarray(np.asarray(w_down[0], f)),
        "gfin": np.ascontiguousarray(np.asarray(final_norm_g, f)[None, :]),
        "ident": ident,
        "rperm": rperm,
        "masks": masks,
        "invf": invf,
    }


def run(x, c, positions, shared, n_cores, S):
    nc = build_nc(S)
    in_maps = []
    for b in range(n_cores):
        m = dict(shared)
        m["x"] = np.ascontiguousarray(np.asarray(x[b], np.float32))
        m["pos"] = np.ascontiguousarray(np.asarray(positions[b], np.int32)[None, :])
        m["ccol"] = _col(np.asarray(c[b], np.float32))
        in_maps.append(m)
    res = run_bass_kernel_spmd(nc, in_maps, core_ids=list(range(n_cores)))
    return np.stack([np.asarray(r["out"], np.float32) for r in res.results], axis=0), res


def kernel(x, c, positions, ada_w, ada_b, norm1_g, w_in, b_in, conv_w, attn_sinks, out_norm_attn_g, out_norm_conv_g,
           w_o, norm2_g, w_up, ffn_conv_w, ffn_conv_b, w_down, final_norm_g):
    x = np.asarray(x)
    B, S, _ = x.shape
    shared = _prep_shared(ada_w, ada_b, norm1_g, w_in, b_in, conv_w, attn_sinks, out_norm_attn_g, out_norm_conv_g, w_o,
                          norm2_g, w_up, ffn_conv_w, ffn_conv_b, w_down, final_norm_g)
    out, _ = run(x, np.asarray(c), np.asarray(positions), shared, B, S)
    return out.astype(np.float32)
```

```python
import math
from contextlib import ExitStack

import numpy as np
import concourse.bass as bass
import concourse.mybir as mybir
from concourse.bass_utils import run_bass_kernel_spmd

F32 = mybir.dt.float32
BF16 = mybir.dt.bfloat16
I32 = mybir.dt.int32
AF = mybir.ActivationFunctionType
ALU = mybir.AluOpType

D = 1024
DFF = 2816
NJ = DFF // 128
T = 512
EPS = 1e-6
C1 = 6.28125
C2 = 2 * math.pi - 6.28125
NWI = 2
NWU = 3
NWD = 8
A_KIND = ["q", "q", "q", "q", "k"] + ["xs", "gb", "gc"] * 4


class Sem:
    def __init__(self, handle, name):
        self.handle = handle
        self.name = name
        self.count = 0


class Eng:
    def __init__(self, name, sem, same_sync):
        self.name = name
        self.sem = sem
        self.items = []
        self.known = {}
        self.same_sync = same_sync


class Sched:
    def __init__(self, nc, stack):
        self.nc = nc
        self.stack = stack
        self.cells = {}
        self.nsem = 0

    def newsem(self, name):
        self.nsem += 1
        return Sem(self.stack.enter_context(self.nc.semaphore(name)), name)

    def neweng(self, name, same_sync):
        return Eng(name, self.newsem("c_" + name), same_sync)

    def _st(self, cell):
        st = self.cells.get(cell)
        if st is None:
            st = {"w": None, "r": {}}
            self.cells[cell] = st
        return st

    def op(self, eng, fn, r=(), w=(), dsem=None, group=None):
        deps = {}

        def add(tok):
            if tok is None:
                return
            s, v = tok[0], tok[1]
            if deps.get(s, 0) < v:
                deps[s] = v

        for c in r:
            add(self._st(c)["w"])
        for c in w:
            st = self._st(c)
            add(st["w"])
            for s, v in st["r"].items():
                add((s, v))
        waits = []
        for s, v in deps.items():
            if s is eng.sem and dsem is None and not eng.same_sync:
                continue
            if eng.known.get(s, 0) >= v:
                continue
            eng.known[s] = v
            waits.append((s, v))
        if dsem is None:
            eng.sem.count += 1
            tok = [eng.sem, eng.sem.count]
            inc = (eng.sem, 1)
        else:
            dsem.count += 16
            tok = [dsem, dsem.count]
            inc = (dsem, 16)
            if group is not None:
                group.append(tok)
        for c in r:
            st = self._st(c)
            if st["r"].get(tok[0], 0) < tok[1]:
                st["r"][tok[0]] = tok[1]
        for c in w:
            st = self._st(c)
            st["w"] = tok
            st["r"] = {}
        eng.items.append((waits, fn, inc))
        return tok

    @staticmethod
    def close_group(group, sem):
        for tok in group:
            tok[1] = sem.count


def replay(eng, e):
    for waits, fn, inc in eng.items:
        for s, v in waits:
            e.wait_ge(s.handle, v)
        ins = fn(e)
        ins.then_inc(inc[0].handle, inc[1])


def build_nc(S):
    NCH = S // T
    nc = bass.Bass("TRN2", target_bir_lowering=False)

    def din(name, shape, dt=F32):
        return nc.dram_tensor(name, shape, dt, kind="ExternalInput").ap()

    x_d = din("x", [S, D])
    pos_d = din("pos", [1, S], I32)
    ccol_d = din("ccol", [128, 8])
    adaw_d = din("ada_w", [D, 6 * D])
    adabc_d = din("adab_col", [128, 32])
    adabg_d = din("adab_g", [2, D])
    n1g_d = din("n1g", [128, 8])
    n2g_d = din("n2g", [128, 8])
    winl_d = din("w_in_l", [9, 128, 8 * 256])
    bcol_d = din("bcol", [128, 17])
    bv_d = din("bv", [1, 128])
    cw_d = din("cw", [128, 12])
    sinks_d = din("sinks", [1, 8])
    gattn_d = din("gattn", [1, 512])
    gconv_d = din("gconv", [128, 4])
    wo_d = din("w_o", [D, D])
    wupl_d = din("w_up_l", [NJ, 128, 8 * 256])
    fcw_d = din("fcw", [128, NJ * 3])
    fcb_d = din("fcb", [128, NJ])
    wdn_d = din("w_down", [DFF, D])
    gfin_d = din("gfin", [1, D])
    ident_d = din("ident", [128, 128])
    rperm_d = din("rperm", [128, 128])
    masks_d = din("masks", [128, 256])
    invf_d = din("invf", [128, 1])
    out_d = nc.dram_tensor("out", [S, D], F32, kind="ExternalOutput").ap()
    ws_bf = nc.dram_tensor("ws_bf", [9 + NJ, 128, 8 * 256], BF16, kind="Internal").ap()
    wo_bf = nc.dram_tensor("wo_bf", [D, D], BF16, kind="Internal").ap()
    wdn_bf = nc.dram_tensor("wdn_bf", [DFF, D], BF16, kind="Internal").ap()

    with ExitStack() as stack:
        def sb(name, shape, dt=F32):
            return stack.enter_context(nc.sbuf_tensor("sb_" + name, shape, dt))

        Sx = Sched(nc, stack)
        PE = Sx.neweng("pe", False)
        ACT = Sx.neweng("act", True)
        DVE = Sx.neweng("dve", True)
        POOL = Sx.neweng("pool", False)
        SP = Sx.neweng("sp", False)

        ps = [stack.enter_context(nc.psum_tensor(f"ps{i}", [128, 512], F32)) for i in range(8)]
        psb = [p[:].bitcast(BF16) for p in ps]

        xin = [sb(f"xin{i}", [128, 4, D]) for i in range(2)]
        xn = [sb(f"xn{i}", [128, D], BF16) for i in range(2)]
        hTm = sb("hTm", [128, 8, T], BF16)
        hTf = sb("hTf", [128, 8, T], BF16)
        wsi = [sb(f"wsi{i}", [128, 8, 256], BF16) for i in range(NWI)]
        wsu = [sb(f"wsu{i}", [128, 8, 256], BF16) for i in range(NWU)]
        wdn = [sb(f"wdn{i}", [128, 512], BF16) for i in range(NWD)]
        Wo_sb = sb("Wo_sb", [128, 8, D], BF16)
        qkraw = sb("qkraw", [128, 5, T], BF16)
        qr = sb("qr", [128, 4, T], BF16)
        kz = [[sb(f"kz{s}{g}", [128, T], BF16) for g in range(2)] for s in range(2)]
        posi = sb("posi", [128, T], I32)
        cosT = sb("cosT", [128, T])
        sinT = sb("sinT", [128, T])
        t1 = sb("t1", [128, T])
        t2 = sb("t2", [128, T])
        Vaug = [sb(f"Vaug{s}", [128, 4, 2, 65], BF16) for s in range(2)]
        maskb = [sb(f"maskb{i}", [128, T], BF16) for i in range(2)]
        pT = [sb(f"pT{i}", [128, T], BF16) for i in range(8)]
        yat = sb("yat", [128, 512])
        ynb = [sb(f"ynb{i}", [128, 512], BF16) for i in range(2)]
        xs_sb = [sb(f"xs_sb{i}", [128, T]) for i in range(2)]
        ub = [sb(f"ub{i}", [128, T + 2]) for i in range(2)]
        ucar = sb("ucar", [128, 4, 2])
        cbuf = [sb(f"cbuf{i}", [128, T]) for i in range(2)]
        yconv = sb("yconv", [128, 4, T])
        ysq = [sb(f"ysq{i}", [128, T], BF16) for i in range(2)]
        rstd_c = sb("rstd_c", [128, T])
        rc_v = sb("rc_v", [128, T])
        rc_t = sb("rc_t", [128, T])
        ang, kf, rr, absr = rc_v, rc_t, cbuf[0], cbuf[1]
        yT = sb("yT", [128, 8, T], BF16)
        a1b = [sb(f"a1b{i}", [128, T]) for i in range(2)]
        fT = sb("fT", [128, NJ, T], BF16)
        gcar = sb("gcar", [128, NJ, 2])
        stt_ = sb("stt_", [128, 64])
        rs_v = sb("rs_v", [128, 64])
        rs_y = sb("rs_y", [128, 64])
        rs_t = sb("rs_t", [128, 64])
        den = sb("den", [128, 8])
        rec = sb("rec", [128, 8])
        junk_m = cbuf[1][:].bitcast(BF16)
        junk_f = a1b[1][:].bitcast(BF16)
        ccol = sb("ccol", [128, 8])
        ca = sb("ca", [128, 8])
        ca_bf = sb("ca_bf", [128, 8], BF16)
        ca_rep = qr[:].rearrange("p a t -> p (a t)")[:, 0:1024].rearrange("p (k m) -> p k m", k=8)
        adabc = sb("adabc", [128, 32])
        modc = sb("modc", [128, 32])
        n1g = sb("n1g", [128, 8])
        n2g = sb("n2g", [128, 8])
        a1c = sb("a1c", [128, 8])
        a2c = sb("a2c", [128, 8])
        fT32 = fT[:].rearrange("p j t -> p (j t)").bitcast(F32)
        g_b = [fT32[:, 0:1024], fT32[:, 1024:2048]]
        g_cells = [[("fT", j) for j in range(0, 4)], [("fT", j) for j in range(4, 8)]]
        wd_stg = [fT32[:, 2048:3072], fT32[:, 3072:4096]]
        wd_stg_cells = [[("fT", j) for j in range(8, 12)], [("fT", j) for j in range(12, 16)]]
        wd_ob = [fT32[:, 4096:4608].bitcast(BF16), fT32[:, 4608:5120].bitcast(BF16)]
        wd_ob_cells = [[("fT", 16), ("fT", 17)], [("fT", 18), ("fT", 19)]]
        wo_stg = yT[:].rearrange("p k t -> p (k t)").bitcast(F32).rearrange("p (k n) -> p k n", k=2)
        yT_all = [("yT", k, tt) for k in range(8) for tt in range(4)]
        bcol = sb("bcol", [128, 17])
        bv_b = sb("bv_b", [128, 128])
        cw = sb("cw", [128, 12])
        sinks_b = sb("sinks_b", [128, 8])
        esink = sb("esink", [128, 8])
        gattn_b = sb("gattn_b", [128, 512])
        gconv = sb("gconv", [128, 4])
        fcw = sb("fcw", [128, NJ * 3])
        fcb = sb("fcb", [128, NJ])
        gfin_b = sb("gfin_b", [128, D])
        cst_f = a1b[0]
        ident_b = sb("ident_b", [128, 128], BF16)
        rperm_b = sb("rperm_b", [128, 128], BF16)
        ones_b = sb("ones_b", [128, 128], BF16)
        invf = sb("invf", [128, 1])
        halfpi = sb("halfpi", [128, 1])

        def mm(out, lhsT, rhs, start, stop, r, w):
            Sx.op(PE, lambda e: e.matmul(out, lhsT=lhsT, rhs=rhs, start=start, stop=stop), r, w)

        def tr(out, in_, r, w):
            Sx.op(PE, lambda e: e.transpose(out, in_, ident_b[:]), list(r) + [("ident_b",)], w)

        def act(out, in_, func, r, w, bias=None, scale=None, accum=None):
            kw = {}
            if bias is not None:
                kw["bias"] = bias
            if scale is not None:
                kw["scale"] = scale
            if accum is not None:
                kw["accum_out"] = accum
            Sx.op(ACT, lambda e: e.activation(out=out, in_=in_, func=func, **kw), r, w)

        def vtt(out, in0, in1, op, r, w, eng=None):
            Sx.op(eng or DVE, lambda e: e.tensor_tensor(out=out, in0=in0, in1=in1, op=op), r, w)

        def vts(out, in0, s1, s2, op0, op1, r, w):
            if op1 is None:
                Sx.op(DVE, lambda e: e.tensor_scalar(out=out, in0=in0, scalar1=s1, scalar2=None, op0=op0), r, w)
            else:
                Sx.op(DVE, lambda e: e.tensor_scalar(out=out, in0=in0, scalar1=s1, scalar2=s2, op0=op0, op1=op1), r, w)

        def vstt(out, in0, scalar, in1, op0, op1, r, w):
            Sx.op(DVE, lambda e: e.scalar_tensor_tensor(out=out, in0=in0, scalar=scalar, in1=in1, op0=op0, op1=op1), r, w)

        def vcopy(out, in_, r, w):
            Sx.op(DVE, lambda e: e.tensor_copy(out=out, in_=in_), r, w)

        def vmemset(ap, val, w):
            Sx.op(DVE, lambda e: e.memset(ap, val), (), w)

        def dma(q, out, in_, sem, r, w, group=None):
            return Sx.op(q, lambda e: e.dma_start(out=out, in_=in_), r, w, dsem=sem, group=group)

        def rsqrt_col(cells, vcol, ycol, tcol, iters=2, single=False):
            rw = list(cells)
            vts(ycol.bitcast(I32), vcol.bitcast(I32), -0.5, 1597463007.0, ALU.mult, ALU.add, rw, rw)
            for _ in range(iters):
                if single:
                    vstt(tcol, ycol, vcol, ycol, ALU.mult, ALU.mult, rw, rw)
                else:
                    vtt(tcol, ycol, ycol, ALU.mult, rw, rw)
                    vtt(tcol, tcol, vcol, ALU.mult, rw, rw)
                vts(tcol, tcol, -0.5, 1.5, ALU.mult, ALU.add, rw, rw)
                vtt(ycol, ycol, tcol, ALU.mult, rw, rw)

        sem_small = Sx.newsem("d_small")
        grp = []

        def lsmall(t_ap, d_ap, cell):
            dma(SP, t_ap, d_ap, sem_small, [], cell if isinstance(cell, list) else [cell], group=grp)

        lsmall(ccol[:], ccol_d[:, :], ("ccol",))
        lsmall(adabc[:], adabc_d[:, :], ("adabc",))
        lsmall(n1g[:], n1g_d[:, :], ("n1g",))
        lsmall(n2g[:], n2g_d[:, :], ("n2g",))
        lsmall(bcol[:], bcol_d[:, :], ("bcol",))
        lsmall(bv_b[:], bv_d[0:1, :].partition_broadcast(128), ("bv_b",))
        lsmall(cw[:], cw_d[:, :], ("cw",))
        lsmall(sinks_b[:], sinks_d[0:1, :].partition_broadcast(128), ("sinks_b",))
        lsmall(gattn_b[:], gattn_d[0:1, :].partition_broadcast(128), ("gattn_b",))
        lsmall(gconv[:], gconv_d[:, :], ("gconv",))
        lsmall(fcw[:], fcw_d[:, :], ("fcw",))
        lsmall(fcb[:], fcb_d[:, :], ("fcb",))
        lsmall(gfin_b[:], gfin_d[0:1, :].partition_broadcast(128), ("gfin_b",))
        lsmall(cst_f[:, 0:128], ident_d[:, :], ("a1b", 0))
        lsmall(cst_f[:, 128:256], rperm_d[:, :], ("a1b", 0))
        lsmall(cst_f[:, 256:512], masks_d[:, :], ("a1b", 0))
        lsmall(invf[:], invf_d[:, :], ("invf",))
        lsmall(g_b[0], adabg_d[0:1, :].partition_broadcast(128), g_cells[0])
        lsmall(g_b[1], adabg_d[1:2, :].partition_broadcast(128), g_cells[1])
        Sched.close_group(grp, sem_small)

        cast_n = {"i": 0}

        def new_cast_sem():
            cast_n["i"] += 1
            return Sx.newsem(f"d_cast{cast_n['i']}")

        sem_ada = [Sx.newsem(f"d_ada{i}") for i in range(2)]
        xin_cells = lambda s: [("xin", s, tt, h) for tt in range(4) for h in range(2)]
        adasb = [xin[i][:].rearrange("p a b -> p (a b)").bitcast(BF16).rearrange("p (k n) -> p k n", k=8)
                 for i in range(2)]

        def ada_piece(v):
            s = v % 2
            dma(POOL, adasb[s], adaw_d[:, v * D:(v + 1) * D].rearrange("(k p) n -> p k n", p=128),
                sem_ada[s], [], xin_cells(s))

        vcopy(ident_b[:], cst_f[:, 0:128], [("a1b", 0)], [("ident_b",)])
        vcopy(rperm_b[:], cst_f[:, 128:256], [("a1b", 0)], [("rperm_b",)])
        for i_ in range(2):
            vts(maskb[i_][:].rearrange("p (h q) -> p h q", h=4),
                cst_f[:, 256 + i_ * 128:256 + (i_ + 1) * 128].unsqueeze(1).broadcast_to([128, 4, 128]),
                -1.0, 30000.0, ALU.add, ALU.mult, [("a1b", 0)], [("maskb",)])
        vmemset(ones_b[:], 1.0, [("ones_b",)])
        vmemset(halfpi[:], math.pi / 2, [("halfpi",)])
        vmemset(ucar[:].rearrange("p a b -> p (a b)"), 0.0, [("ucar", cc) for cc in range(4)])
        vmemset(gcar[:].rearrange("p a b -> p (a b)"), 0.0, [("gcar", j) for j in range(NJ)])
        for s in range(2):
            vmemset(Vaug[s][:].rearrange("p a b c -> p (a b c)"), 1.0, [("Vaug", s, tt) for tt in range(4)])
            for g in range(2):
                vmemset(kz[s][g][:], 0.0, [("kz", s, g)])
        act(ca[:], ccol[:], AF.Silu, [("ccol",)], [("ca",)])
        act(esink[:], sinks_b[:], AF.Exp, [("sinks_b",)], [("esink",)])
        vcopy(ca_bf[:], ca[:], [("ca",)], [("ca_bf",)])
        qr_all = [("qr", a_, q_) for a_ in range(4) for q_ in range(4)]
        vcopy(ca_rep, ca[:, :].unsqueeze(2).broadcast_to([128, 8, 128]), [("ca",)], qr_all)

        for i0 in range(0, 9, 3):
            dma(POOL, ws_bf[i0:i0 + 3], winl_d[i0:i0 + 3], new_cast_sem(), [], [("ws", i) for i in range(i0, i0 + 3)])
        ada_piece(0)
        ada_piece(1)
        colv = {0: 0, 1: 1, 3: 2, 4: 3}
        for v in range(6):
            s = v % 2
            if v in colv:
                for oc in range(8):
                    col = colv[v] * 8 + oc
                    for kc in range(8):
                        mm(ps[0][:, col:col + 1], adasb[s][:, kc, oc * 128:(oc + 1) * 128], ca_bf[:, kc:kc + 1],
                           kc == 0, kc == 7, xin_cells(s) + [("ca_bf",)], [("ps", 0)])
            else:
                gi = 0 if v == 2 else 1
                for h in range(2):
                    for kc in range(8):
                        mm(ps[1 + h][:, :], ca_rep[:, kc, :], adasb[s][:, kc, h * 512:(h + 1) * 512],
                           kc == 0, kc == 7, xin_cells(s) + qr_all, [("ps", 1 + h)])
                    vtt(g_b[gi][:, h * 512:(h + 1) * 512], ps[1 + h][:, :], g_b[gi][:, h * 512:(h + 1) * 512],
                        ALU.add, [("ps", 1 + h)] + g_cells[gi], g_cells[gi])
            if v + 2 < 6:
                ada_piece(v + 2)
        vtt(modc[:], ps[0][:, 0:32], adabc[:], ALU.add, [("ps", 0), ("adabc",)], [("modc",)])
        vts(a1c[:], modc[:, 8:16], 1.0, None, ALU.add, None, [("modc",)], [("a1c",)])
        vtt(a1c[:], a1c[:], n1g[:], ALU.mult, [("a1c",), ("n1g",)], [("a1c",)])
        vts(a2c[:], modc[:, 24:32], 1.0, None, ALU.add, None, [("modc",)], [("a2c",)])
        vtt(a2c[:], a2c[:], n2g[:], ALU.mult, [("a2c",), ("n2g",)], [("a2c",)])
        sh1 = modc[:, 0:8]
        sh2 = modc[:, 16:24]

        for j0 in range(0, NJ, 4):
            j1 = min(NJ, j0 + 4)
            dma(POOL, ws_bf[9 + j0:9 + j1], wupl_d[j0:j1], new_cast_sem(), [], [("ws", 9 + j) for j in range(j0, j1)])

        sem_wo = Sx.newsem("d_wo")
        for q4 in range(4):
            dma(SP, wo_stg, wo_d[q4 * 256:(q4 + 1) * 256, :].rearrange("(k p) n -> p k n", p=128), sem_wo, [], yT_all)
            vtt(Wo_sb[:, 2 * q4:2 * q4 + 2, :], wo_stg, g_b[0].unsqueeze(1).broadcast_to([128, 2, D]), ALU.mult,
                yT_all + g_cells[0], [("Wo_sb",)])
        sem_wds = [Sx.newsem(f"d_wds{i}") for i in range(2)]
        sem_wdo = [Sx.newsem(f"d_wdo{i}") for i in range(2)]
        for j in range(NJ):
            s = j % 2
            dma(SP, wd_stg[s], wdn_d[j * 128:(j + 1) * 128, :], sem_wds[s], [], wd_stg_cells[s])
            vtt(wd_ob[s], wd_stg[s], g_b[1], ALU.mult, wd_stg_cells[s] + g_cells[1], wd_ob_cells[s])
            dma(SP, wdn_bf[j * 128:(j + 1) * 128, :], wd_ob[s], sem_wdo[s], wd_ob_cells[s], [("wdnbf", j)])

        sem_x = [[Sx.newsem(f"d_x{i}{h}") for h in range(2)] for i in range(2)]
        sem_pos = Sx.newsem("d_pos")
        sem_out = [[Sx.newsem(f"d_out{i}{h}") for h in range(2)] for i in range(2)]
        sem_wi = [Sx.newsem(f"d_wi{i}") for i in range(NWI)]
        sem_wu = [Sx.newsem(f"d_wu{i}") for i in range(NWU)]
        sem_wd = [Sx.newsem(f"d_wd{i}") for i in range(NWD)]

        def load_x(c, half):
            s = c % 2
            t0 = c * T + half * 256
            dma(POOL, xin[s][:, half * 2:half * 2 + 2, :], x_d[t0:t0 + 256, :].rearrange("(t p) d -> p t d", p=128),
                sem_x[s][half], [], [("xin", s, tt, h) for tt in (half * 2, half * 2 + 1) for h in range(2)])

        def load_pos(c):
            dma(POOL, posi[:], pos_d[0:1, c * T:(c + 1) * T].partition_broadcast(128), sem_pos, [], [("posi",)])

        def make_ring(seq, bufs, sems, cellname, src):
            st = {"issued": 0, "used": 0}
            n = len(bufs)

            def ensure(upto):
                while st["issued"] < min(len(seq), upto):
                    i = st["issued"]
                    s = i % n
                    src(seq[i], bufs[s], sems[s], (cellname, s))
                    st["issued"] += 1

            def nxt():
                i = st["used"]
                ensure(i + n)
                st["used"] += 1
                return i % n

            return nxt

        wi_next = make_ring([p for c in range(NCH) for p in range(9)], wsi, sem_wi, "wsi",
                            lambda p, buf, sem, cell: dma(SP, buf[:], ws_bf[p].rearrange("p (k n) -> p k n", k=8), sem,
                                                          [("ws", p)], [cell]))
        wu_next = make_ring([9 + j for c in range(NCH) for j in range(NJ)], wsu, sem_wu, "wsu",
                            lambda p, buf, sem, cell: dma(SP, buf[:], ws_bf[p].rearrange("p (k n) -> p k n", k=8), sem,
                                                          [("ws", p)], [cell]))
        wd_next = make_ring([(j, ph) for c in range(NCH) for ph in range(2) for j in range(NJ)], wdn, sem_wd, "wdn",
                            lambda jp, buf, sem, cell: dma(SP, buf[:], wdn_bf[jp[0] * 128:(jp[0] + 1) * 128,
                                                                              jp[1] * 512:(jp[1] + 1) * 512], sem,
                                                           [("wdnbf", jp[0])], [cell]))

        rp = {"i": 0, "banks": [5, 6]}

        def rp_next():
            b = rp["banks"][rp["i"] % len(rp["banks"])]
            rp["i"] += 1
            return b

        stc = {"i": 0}

        def st_block(n):
            if stc["i"] % 64 + n > 64:
                stc["i"] += 64 - stc["i"] % 64
            i = stc["i"] % 64
            stc["i"] += n
            return i

        def rstd_block(col, n, n_inv):
            cells = [("rs", col + i) for i in range(n)]
            vts(rs_v[:, col:col + n], stt_[:, col:col + n], n_inv, EPS, ALU.mult, ALU.add,
                [("st", col + i) for i in range(n)], cells)
            rsqrt_col(cells, rs_v[:, col:col + n], rs_y[:, col:col + n], rs_t[:, col:col + n], single=False)

        def staged(n, stages, cost):
            ns = len(stages)
            for t in range(n + ns - 1):
                for s_ in range(ns - 1, -1, -1):
                    k = t - s_
                    if 0 <= k < n:
                        stages[s_](k)
                yield cost

        def norm_to_hT(slot, hT, hname, ac, sh, ac_cell, sh_cell, banks, junk, junk_cell):
            col = st_block(4)
            for tt in range(4):
                xc = [("xin", slot, tt, 0), ("xin", slot, tt, 1)]
                act(junk, xin[slot][:, tt, :], AF.Square, xc, [junk_cell, ("st", col + tt)], accum=stt_[:, col + tt:col + tt + 1])
            yield 2.0
            rstd_block(col, 4, 1.0 / D)
            yield 2.0

            def s_xn(tt):
                xc = [("xin", slot, tt, 0), ("xin", slot, tt, 1)]
                vts(xn[tt % 2][:], xin[slot][:, tt, :], rs_y[:, col + tt:col + tt + 1], None, ALU.mult, None,
                    xc + [("rs", col + tt)], [("xn", tt % 2)])

            def s_tr(tt):
                bank = banks[tt % len(banks)]
                for kc in range(8):
                    tr(psb[bank][:, kc * 128:(kc + 1) * 128], xn[tt % 2][:, kc * 128:(kc + 1) * 128], [("xn", tt % 2)],
                       [("ps", bank)])

            def s_ev(tt):
                bank = banks[tt % len(banks)]
                for kc in range(8):
                    if kc % 8 < 8:
                        act(hT[:, kc, tt * 128:(tt + 1) * 128], psb[bank][:, kc * 128:(kc + 1) * 128], AF.Identity,
                            [("ps", bank), ac_cell, sh_cell], [(hname, kc, tt)], bias=sh[:, kc:kc + 1], scale=ac[:, kc:kc + 1])
                    else:
                        vts(hT[:, kc, tt * 128:(tt + 1) * 128], psb[bank][:, kc * 128:(kc + 1) * 128], ac[:, kc:kc + 1],
                            sh[:, kc:kc + 1], ALU.mult, ALU.add, [("ps", bank), ac_cell, sh_cell], [(hname, kc, tt)])

            yield from staged(4, [s_xn, s_tr, s_ev], 2.5)

        def mixer(c):
            slot = c % 2
            sK = c % 2
            vcopy(kf[:], posi[:], [("posi",)], [("rc",)])
            vts(ang[:], kf[:], invf[:, 0:1], None, ALU.mult, None, [("rc",), ("invf",)], [("rc",)])
            vts(t1[:], ang[:], 1.0 / (2 * math.pi), None, ALU.mult, None, [("rc",)], [("t1",)])
            vcopy(posi[:], t1[:], [("t1",)], [("posi",)])
            vcopy(kf[:], posi[:], [("posi",)], [("rc",)])
            yield 2.0
            vstt(rr[:], kf[:], -C1, ang[:], ALU.mult, ALU.add, [("rc",)], [("cbuf", 0)])
            vstt(rr[:], kf[:], -C2, rr[:], ALU.mult, ALU.add, [("rc",), ("cbuf", 0)], [("cbuf", 0)])
            vts(rr[:], rr[:], 3.1415925, -3.1415925, ALU.min, ALU.max, [("cbuf", 0)], [("cbuf", 0)])
            yield 2.0
            act(sinT[:], rr[:], AF.Sin, [("cbuf", 0)], [("sinT",)])
            act(absr[:], rr[:], AF.Abs, [("cbuf", 0)], [("cbuf", 1)])
            act(cosT[:], absr[:], AF.Sin, [("cbuf", 1), ("halfpi",)], [("cosT",)], bias=halfpi[:, 0:1], scale=-1.0)
            if c + 1 < NCH:
                load_pos(c + 1)
            yield 1.0
            yield from norm_to_hT(slot, hTm, "hTm", a1c, sh1, ("a1c",), ("modc",), [4, 5, 6, 7], junk_m, ("cbuf", 1))

            rp["banks"] = [5, 6, 4]
            info = {}

            def win_mm(a):
                if a < 17:
                    if a % 2 == 0:
                        info["cur"] = wi_next()
                    cur = info["cur"]
                    sub = a % 2
                    bank = rp_next()
                    info[a] = bank
                    for kc in range(8):
                        mm(ps[bank][:, :], wsi[cur][:, kc, sub * 128:(sub + 1) * 128], hTm[:, kc, :], kc == 0, kc == 7,
                           [("wsi", cur)] + [("hTm", kc, tt) for tt in range(4)], [("ps", bank)])
                else:
                    tt = a - 17
                    cur = info["cur"]
                    bank = rp_next()
                    info[a] = bank
                    for kc in range(8):
                        mm(ps[bank][:, 0:128], hTm[:, kc, tt * 128:(tt + 1) * 128], wsi[cur][:, kc, 128:256], kc == 0, kc == 7,
                           [("wsi", cur), ("hTm", kc, tt)], [("ps", bank)])

            def win_ev(a):
                bank = info[a]
                if a >= 17:
                    tt = a - 17
                    vtt(Vaug[sK][:, tt, :, 0:64], ps[bank][:, 0:128].rearrange("p (g d) -> p g d", g=2),
                        bv_b[:].rearrange("p (g d) -> p g d", g=2), ALU.add, [("ps", bank), ("bv_b",)], [("Vaug", sK, tt)])
                    return
                kind = A_KIND[a]
                bc = bcol[:, a:a + 1]
                if kind in ("q", "k"):
                    act(qkraw[:, a, :], ps[bank][:, :], AF.Identity, [("ps", bank), ("bcol",)], [("qkraw", a)], bias=bc)
                    return
                cc = (a - 5) // 3
                s2 = cc % 2
                if kind == "xs":
                    act(xs_sb[s2][:], ps[bank][:, :], AF.Identity, [("ps", bank), ("bcol",)], [("xs_sb", s2)], bias=bc)
                elif kind == "gb":
                    vcopy(ub[s2][:, 0:2], ucar[:, cc, :], [("ucar", cc)], [("ub", s2)])
                    vstt(ub[s2][:, 2:T + 2], ps[bank][:, :], bc, xs_sb[s2][:], ALU.add, ALU.mult,
                         [("ps", bank), ("bcol",), ("xs_sb", s2)], [("ub", s2)])
                    vcopy(ucar[:, cc, :], ub[s2][:, T:T + 2], [("ub", s2)], [("ucar", cc)])
                    act(cbuf[s2][:], ub[s2][:, 2:T + 2], AF.Copy, [("ub", s2), ("cw",)], [("cbuf", s2)],
                        scale=cw[:, cc * 3 + 2:cc * 3 + 3])
                    vstt(cbuf[s2][:], ub[s2][:, 1:T + 1], cw[:, cc * 3 + 1:cc * 3 + 2], cbuf[s2][:], ALU.mult, ALU.add,
                         [("ub", s2), ("cw",), ("cbuf", s2)], [("cbuf", s2)])
                    vstt(cbuf[s2][:], ub[s2][:, 0:T], cw[:, cc * 3:cc * 3 + 1], cbuf[s2][:], ALU.mult, ALU.add,
                         [("ub", s2), ("cw",), ("cbuf", s2)], [("cbuf", s2)])
                else:
                    vstt(yconv[:, cc, :], ps[bank][:, :], bc, cbuf[s2][:], ALU.add, ALU.mult,
                         [("ps", bank), ("bcol",), ("cbuf", s2)], [("yconv", cc)])
                    act(ysq[s2][:], yconv[:, cc, :], AF.Square, [("yconv", cc)], [("ysq", s2)])

            def win_ss(a):
                if a < 17 and A_KIND[a] == "gc":
                    cc = (a - 5) // 3
                    s2 = cc % 2
                    mm(ps[7][:, :], ones_b[:], ysq[s2][:], cc == 0, cc == 3, [("ones_b",), ("ysq", s2)], [("ps", 7)])

            yield from staged(21, [win_mm, win_ev, win_ss], 2.2)

            def rope_mm(a):
                bank = rp_next()
                info[("r", a)] = bank
                mm(ps[bank][:, :], rperm_b[:], qkraw[:, a, :], True, True, [("rperm_b",), ("qkraw", a)], [("ps", bank)])

            def rope_ev(a):
                bank = info[("r", a)]
                vtt(t1[:], qkraw[:, a, :], cosT[:], ALU.mult, [("qkraw", a), ("cosT",)], [("t1",)])
                vtt(t2[:], ps[bank][:, :], sinT[:], ALU.mult, [("ps", bank), ("sinT",)], [("t2",)])
                if a < 4:
                    vtt(qr[:, a, :], t1[:], t2[:], ALU.add, [("t1",), ("t2",)], [("qr", a, qb) for qb in range(4)])
                else:
                    vtt(kz[sK][0][0:64, :], t1[0:64, :], t2[0:64, :], ALU.add, [("t1",), ("t2",)], [("kz", sK, 0)])
                    vtt(kz[sK][1][64:128, :], t1[64:128, :], t2[64:128, :], ALU.add, [("t1",), ("t2",)], [("kz", sK, 1)])

            yield from staged(5, [rope_mm, rope_ev], 1.8)

            vts(rc_v[:], ps[7][:, :], 1.0 / 512, EPS, ALU.mult, ALU.add, [("ps", 7)], [("rc",)])
            rw = [("rc",)]
            vts(rstd_c[:].bitcast(I32), rc_v[:].bitcast(I32), -0.5, 1597463007.0, ALU.mult, ALU.add, rw, rw)
            yield 1.2
            for _ in range(2):
                vtt(rc_t[:], rstd_c[:], rstd_c[:], ALU.mult, rw, rw)
                vtt(rc_t[:], rc_t[:], rc_v[:], ALU.mult, rw, rw)
                yield 1.2
                vts(rc_t[:], rc_t[:], -0.5, 1.5, ALU.mult, ALU.add, rw, rw)
                vtt(rstd_c[:], rstd_c[:], rc_t[:], ALU.mult, rw, rw)
                yield 1.2
            for cc in range(4):
                vstt(yT[:, 4 + cc, :], yconv[:, cc, :], gconv[:, cc:cc + 1], rstd_c[:], ALU.mult, ALU.mult,
                     [("yconv", cc), ("gconv",), ("rc",)], [("yT", 4 + cc, tt) for tt in range(4)])
                if cc % 2 == 1:
                    yield 1.2

            rp["banks"] = [5, 6]
            OB = [7, 4]
            att = {}

            def kv_src(qb, kb):
                if kb == "cur":
                    return sK, qb
                if qb > 0:
                    return sK, qb - 1
                return 1 - sK, 3

            def a_scores(qb):
                n = c * 4 + qb
                kbs = ([] if n == 0 else ["prev"]) + ["cur"]
                tiles = []
                for g in range(2):
                    for kb in kbs:
                        ksl, kcol = kv_src(qb, kb)
                        bank = rp_next()
                        mm(ps[bank][:, :], kz[ksl][g][:, kcol * 128:(kcol + 1) * 128],
                           qr[:, :, qb * 128:(qb + 1) * 128], True, False,
                           [("kz", ksl, g)] + [("qr", a, qb) for a in range(4)], [("ps", bank)])
                        mb = maskb[0] if kb == "cur" else maskb[1]
                        mm(ps[bank][:, :], ident_b[:], mb[:], False, True, [("ident_b",), ("maskb",)], [("ps", bank)])
                        ti = (qb * 4 + len(tiles)) % 8
                        act(pT[ti][:], ps[bank][:, :], AF.Exp, [("ps", bank)], [("pT", ti)], scale=0.125)
                        tiles.append((g, kb, ti))
                att[qb] = (kbs, tiles)

            def a_pv(qb):
                kbs, tiles = att[qb]
                for g in range(2):
                    po = ps[OB[g]][:, 0:260].rearrange("p (h d) -> p h d", h=4)
                    for i in range(4):
                        for ki, kb in enumerate(kbs):
                            vsl, vt = kv_src(qb, kb)
                            ti = [t_ for (g_, kb_, t_) in tiles if g_ == g and kb_ == kb][0]
                            mm(po[:, i, :], pT[ti][:, i * 128:(i + 1) * 128], Vaug[vsl][:, vt, g, :], ki == 0,
                               ki == len(kbs) - 1, [("pT", ti), ("Vaug", vsl, vt)], [("ps", OB[g])])

            def a_norm(qb):
                for g in range(2):
                    po = ps[OB[g]][:, 0:260].rearrange("p (h d) -> p h d", h=4)
                    vtt(den[:, g * 4:(g + 1) * 4], po[:, :, 64], esink[:, g * 4:(g + 1) * 4], ALU.add,
                        [("ps", OB[g]), ("esink",)], [("den",)])
                Sx.op(DVE, lambda e: e.reciprocal(out=rec[:], in_=den[:]), [("den",)], [("rec",)])
                for g in range(2):
                    po = ps[OB[g]][:, 0:260].rearrange("p (h d) -> p h d", h=4)
                    vtt(yat[:, g * 256:(g + 1) * 256].rearrange("p (h d) -> p h d", h=4), po[:, :, 0:64],
                        rec[:, g * 4:(g + 1) * 4].unsqueeze(2).broadcast_to([128, 4, 64]), ALU.mult,
                        [("ps", OB[g]), ("rec",)], [("yat",)])
                col = st_block(1)
                att[("col", qb)] = col
                act(junk_m[:, 0:512], yat[:], AF.Square, [("yat",)], [("cbuf", 1), ("st", col)], accum=stt_[:, col:col + 1])

            def a_yn(qb):
                col = att[("col", qb)]
                rstd_block(col, 1, 1.0 / 512)
                vstt(ynb[qb % 2][:], yat[:], rs_y[:, col:col + 1], gattn_b[:], ALU.mult, ALU.mult,
                     [("yat",), ("rs", col), ("gattn_b",)], [("ynb", qb % 2)])

            def a_tr(qb):
                bank = rp_next()
                att[("tb", qb)] = bank
                for j in range(4):
                    tr(psb[bank][:, j * 128:(j + 1) * 128], ynb[qb % 2][:, j * 128:(j + 1) * 128], [("ynb", qb % 2)],
                       [("ps", bank)])

            def a_ev(qb):
                bank = att[("tb", qb)]
                act(yT[:, 0:4, qb * 128:(qb + 1) * 128], psb[bank][:, 0:512].rearrange("p (k t) -> p k t", k=4), AF.Copy,
                    [("ps", bank)], [("yT", k, qb) for k in range(4)])

            yield from staged(4, [a_scores, a_pv, a_norm, a_yn, a_tr, a_ev], 2.2)

            def wo_mm(i):
                tt, h = divmod(i, 2)
                bank = rp_next()
                info[("o", i)] = bank
                for kc in range(8):
                    mm(ps[bank][:, :], yT[:, kc, tt * 128:(tt + 1) * 128], Wo_sb[:, kc, h * 512:(h + 1) * 512], kc == 0,
                       kc == 7, [("yT", kc, tt), ("Wo_sb",)], [("ps", bank)])

            def wo_ev(i):
                tt, h = divmod(i, 2)
                bank = info[("o", i)]
                vtt(xin[slot][:, tt, h * 512:(h + 1) * 512], xin[slot][:, tt, h * 512:(h + 1) * 512], ps[bank][:, :], ALU.add,
                    [("ps", bank), ("xin", slot, tt, h)], [("xin", slot, tt, h)])

            yield from staged(8, [wo_mm, wo_ev], 2.2)

            yield from norm_to_hT(slot, hTf, "hTf", a2c, sh2, ("a2c",), ("modc",), [4, 7, 5, 6], junk_m, ("cbuf", 1))

        def ffn(c):
            slot = c % 2
            fi = {}

            def up_mm(j):
                ws_ = wu_next()
                bg, bvv = (0, 1) if j % 2 == 0 else (2, 3)
                for (bk, off) in ((bg, 0), (bvv, 128)):
                    for kc in range(8):
                        mm(ps[bk][:, :], wsu[ws_][:, kc, off:off + 128], hTf[:, kc, :], kc == 0, kc == 7,
                           [("wsu", ws_)] + [("hTf", kc, tt) for tt in range(4)], [("ps", bk)])

            def up_e1(j):
                bg, bvv = (0, 1) if j % 2 == 0 else (2, 3)
                s2 = j % 2
                w0 = fcw[:, j * 3:j * 3 + 1]
                w1 = fcw[:, j * 3 + 1:j * 3 + 2]
                w2 = fcw[:, j * 3 + 2:j * 3 + 3]
                act(a1b[s2][:], ps[bg][:, :], AF.Identity, [("ps", bg), ("fcw",), ("fcb",)], [("a1b", s2)],
                    bias=fcb[:, j:j + 1], scale=w2)
                vstt(a1b[s2][:, 1:T], ps[bg][:, 0:T - 1], w1, a1b[s2][:, 1:T], ALU.mult, ALU.add,
                     [("ps", bg), ("fcw",), ("a1b", s2)], [("a1b", s2)])
                vstt(a1b[s2][:, 2:T], ps[bg][:, 0:T - 2], w0, a1b[s2][:, 2:T], ALU.mult, ALU.add,
                     [("ps", bg), ("fcw",), ("a1b", s2)], [("a1b", s2)])
                vstt(a1b[s2][:, 0:2], gcar[:, j, 0:2], w0, a1b[s2][:, 0:2], ALU.mult, ALU.add,
                     [("gcar", j), ("fcw",), ("a1b", s2)], [("a1b", s2)])
                vstt(a1b[s2][:, 0:1], gcar[:, j, 1:2], w1, a1b[s2][:, 0:1], ALU.mult, ALU.add,
                     [("gcar", j), ("fcw",), ("a1b", s2)], [("a1b", s2)])
                vcopy(gcar[:, j, :], ps[bg][:, T - 2:T], [("ps", bg)], [("gcar", j)])

            def up_e2(j):
                bg, bvv = (0, 1) if j % 2 == 0 else (2, 3)
                s2 = j % 2
                act(a1b[s2][:], a1b[s2][:], AF.Silu, [("a1b", s2)], [("a1b", s2)])
                vtt(fT[:, j, :], a1b[s2][:], ps[bvv][:, :], ALU.mult, [("a1b", s2), ("ps", bvv)], [("fT", j)])

            yield from staged(NJ, [up_mm, up_e1, up_e2], 4.3)
            for ph in range(2):
                for j in range(NJ):
                    wd_ = wd_next()
                    for tt in range(4):
                        mm(ps[tt][:, :], fT[:, j, tt * 128:(tt + 1) * 128], wdn[wd_][:, :], j == 0, j == NJ - 1,
                           [("fT", j), ("wdn", wd_)], [("ps", tt)])
                    yield 1.1
                for tt in range(4):
                    vtt(xin[slot][:, tt, ph * 512:(ph + 1) * 512], xin[slot][:, tt, ph * 512:(ph + 1) * 512], ps[tt][:, :],
                        ALU.add, [("ps", tt), ("xin", slot, tt, ph)], [("xin", slot, tt, ph)])
                    if tt % 2 == 1:
                        yield 1.0
            col = st_block(4)
            for tt in range(4):
                xc = [("xin", slot, tt, 0), ("xin", slot, tt, 1)]
                act(junk_f, xin[slot][:, tt, :], AF.Square, xc, [("a1b", 1), ("st", col + tt)],
                    accum=stt_[:, col + tt:col + tt + 1])
            yield 1.0
            rstd_block(col, 4, 1.0 / D)
            yield 1.0
            for hf in range(2):
                for tt in (hf * 2, hf * 2 + 1):
                    xc = [("xin", slot, tt, 0), ("xin", slot, tt, 1)]
                    vstt(xin[slot][:, tt, :], xin[slot][:, tt, :], rs_y[:, col + tt:col + tt + 1], gfin_b[:], ALU.mult, ALU.mult,
                         xc + [("rs", col + tt), ("gfin_b",)], xc)
                dma(POOL, out_d[c * T + hf * 256:c * T + (hf + 1) * 256, :].rearrange("(t p) d -> p t d", p=128),
                    xin[slot][:, hf * 2:(hf + 1) * 2, :], sem_out[slot][hf],
                    [("xin", slot, tt, h) for tt in (hf * 2, hf * 2 + 1) for h in range(2)], [("out", c, hf)])
                if c + 2 < NCH:
                    load_x(c + 2, hf)
                yield 1.0

        TF = 160.0
        TM = 152.0
        DELTA = 0.10
        MSPAN = 0.84

        def drain(gen):
            for _ in gen:
                pass

        def stream_f():
            for c in range(NCH):
                for cost in ffn(c):
                    yield cost / TF

        def stream_m():
            for k in range(1, NCH):
                yield ("start", k - 1 + DELTA)
                for cost in mixer(k):
                    yield cost / TM * MSPAN

        for half in range(2):
            load_x(0, half)
        load_pos(0)
        if NCH > 1:
            for half in range(2):
                load_x(1, half)
        drain(mixer(0))
        gf, gm = stream_f(), stream_m()
        tf = tm = 0.0
        df = dm = False
        while not (df and dm):
            if not dm and (df or tm <= tf):
                try:
                    it = next(gm)
                    if isinstance(it, tuple):
                        tm = max(tm, it[1])
                    else:
                        tm += it
                except StopIteration:
                    dm = True
            else:
                try:
                    tf += next(gf)
                except StopIteration:
                    df = True

        fin = [(s, s.count) for pair in sem_out for s in pair if s.count > 0]

        block = stack.enter_context(nc.Block())

        @block.tensor
        def _(e):
            replay(PE, e)

        @block.scalar
        def _(e):
            replay(ACT, e)

        @block.vector
        def _(e):
            replay(DVE, e)

        @block.gpsimd
        def _(e):
            replay(POOL, e)
            for s, v in fin:
                e.wait_ge(s.handle, v)

        @block.sync
        def _(e):
            replay(SP, e)

    return nc


def _host_consts():
    ident = np.eye(128, dtype=np.float32)
    rperm = np.zeros((128, 128), dtype=np.float32)
    for m in range(128):
        if (m % 64) < 32:
            rperm[m + 32, m] = -1.0
        else:
            rperm[m - 32, m] = 1.0
    kk = np.arange(128)[:, None]
    qq = np.arange(128)[None, :]
    mask_cur = (qq >= kk).astype(np.float32)
    mask_prev = (qq < kk).astype(np.float32)
    masks = np.concatenate([mask_cur, mask_prev], axis=1)
    inv = (np.float32(10000.0) ** (-(np.arange(32, dtype=np.float32)) / np.float32(32))).astype(np.float32)
    invf = np.tile(inv, 4)[:, None].astype(np.float32)
    return ident, rperm, masks, invf


def _col(v):
    v = np.asarray(v, dtype=np.float32)
    return np.ascontiguousarray(v.reshape(-1, 128).T)


def _prep_shared(ada_w, ada_b, norm1_g, w_in, b_in, conv_w, attn_sinks, out_norm_attn_g, out_norm_conv_g, w_o,
                 norm2_g, w_up, ffn_conv_w, ffn_conv_b, w_down, final_norm_g):
    f = np.float32
    w_in = np.asarray(w_in[0], f)
    b_in = np.asarray(b_in[0], f)
    q0, k0, v0, gb0, gc0, xs0 = 0, 512, 640, 768, 1280, 1792
    hperm = [0, 4, 1, 5, 2, 6, 3, 7]
    cols = []
    for a in range(4):
        for hh in (hperm[2 * a], hperm[2 * a + 1]):
            cols.extend(range(q0 + hh * 64, q0 + (hh + 1) * 64))
    cols.extend(range(k0, k0 + 128))
    for cc in range(4):
        cols.extend(range(xs0 + cc * 128, xs0 + (cc + 1) * 128))
        cols.extend(range(gb0 + cc * 128, gb0 + (cc + 1) * 128))
        cols.extend(range(gc0 + cc * 128, gc0 + (cc + 1) * 128))
    cols.extend(range(v0, v0 + 128))
    cols = np.array(cols)
    wl = w_in[:, cols]
    w_in_l = np.ascontiguousarray(wl.reshape(8, 128, 9, 256).transpose(2, 1, 0, 3)).reshape(9, 128, 2048)
    bl = b_in[cols]
    bcol = _col(bl[:17 * 128])
    bv = np.ascontiguousarray(bl[17 * 128:][None, :])
    w_up = np.asarray(w_up[0], f)
    gate = w_up[:, :DFF].reshape(8, 128, NJ, 128)
    val = w_up[:, DFF:].reshape(8, 128, NJ, 128)
    w_up_l = np.ascontiguousarray(np.concatenate([gate, val], axis=3).transpose(2, 1, 0, 3)).reshape(NJ, 128, 2048)
    ada_b = np.asarray(ada_b[0], f)
    adab_col = np.concatenate([_col(ada_b[v * D:(v + 1) * D]) for v in (0, 1, 3, 4)], axis=1)
    adab_g = np.ascontiguousarray(np.stack([ada_b[2 * D:3 * D], ada_b[5 * D:6 * D]]))
    cw = np.asarray(conv_w[0], f)
    cwl = np.ascontiguousarray(cw.reshape(3, 4, 128).transpose(2, 1, 0)).reshape(128, 12)
    fw = np.asarray(ffn_conv_w[0], f)
    fcw = np.ascontiguousarray(fw.reshape(3, NJ, 128).transpose(2, 1, 0)).reshape(128, NJ * 3)
    fcb = _col(np.asarray(ffn_conv_b[0], f))
    ident, rperm, masks, invf = _host_consts()
    return {
        "ada_w": np.ascontiguousarray(np.asarray(ada_w[0], f)),
        "adab_col": np.ascontiguousarray(adab_col),
        "adab_g": adab_g,
        "n1g": _col(norm1_g[0]),
        "n2g": _col(norm2_g[0]),
        "w_in_l": w_in_l,
        "bcol": bcol,
        "bv": bv,
        "cw": cwl,
        "sinks": np.ascontiguousarray(np.asarray(attn_sinks[0], f)[None, :]),
        "gattn": np.ascontiguousarray(np.asarray(out_norm_attn_g[0], f)[None, :]),
        "gconv": _col(out_norm_conv_g[0]),
        "w_o": np.ascontiguousarray(np.asarray(w_o[0], f)),
        "w_up_l": w_up_l,
        "fcw": fcw,
        "fcb": fcb,
        "w_down": np.ascontiguousarray(np.asarray(w_down[0], f)),
        "gfin": np.ascontiguousarray(np.asarray(final_norm_g, f)[None, :]),
        "ident": ident,
        "rperm": rperm,
        "masks": masks,
        "invf": invf,
    }


def run(x, c, positions, shared, n_cores, S):
    nc = build_nc(S)
    in_maps = []
    for b in range(n_cores):
        m = dict(shared)
        m["x"] = np.ascontiguousarray(np.asarray(x[b], np.float32))
        m["pos"] = np.ascontiguousarray(np.asarray(positions[b], np.int32)[None, :])
        m["ccol"] = _col(np.asarray(c[b], np.float32))
        in_maps.append(m)
    res = run_bass_kernel_spmd(nc, in_maps, core_ids=list(range(n_cores)))
    return np.stack([np.asarray(r["out"], np.float32) for r in res.results], axis=0), res


def kernel(x, c, positions, ada_w, ada_b, norm1_g, w_in, b_in, conv_w, attn_sinks, out_norm_attn_g, out_norm_conv_g,
           w_o, norm2_g, w_up, ffn_conv_w, ffn_conv_b, w_down, final_norm_g):
    x = np.asarray(x)
    B, S, _ = x.shape
    shared = _prep_shared(ada_w, ada_b, norm1_g, w_in, b_in, conv_w, attn_sinks, out_norm_attn_g, out_norm_conv_g, w_o,
                          norm2_g, w_up, ffn_conv_w, ffn_conv_b, w_down, final_norm_g)
    out, _ = run(x, np.asarray(c), np.asarray(positions), shared, B, S)
    return out.astype(np.float32)
```
